# Optimizing a Trainium2 kernel written in Bass

```python
import jax, jax.numpy as jnp
from jax import lax
import numpy as np

D_MODEL = 1024
BATCH = 16
SEQ = 2048
DEPTH = 2

N_MEM = 256
EPS = 1e-6

SB_HEADS = 8
SB_HEAD_DIM = 64
SB_WIDTH = SB_HEADS * SB_HEAD_DIM
SB_BLOCK = 128
SG_GROUPS = 4
SG_GROUP_DIM = 64
SG_WIDTH = SG_GROUPS * SG_GROUP_DIM
SG_CHUNK = 128
GLA_HEADS = 4
GLA_DK = 32
GLA_DV = 64
GLA_KEY_WIDTH = GLA_HEADS * GLA_DK
GLA_WIDTH = GLA_HEADS * GLA_DV
GLA_GATE_RANK = 16
GLA_TAU = 16.0
GLA_CHUNK = 128

MIX_WIDTH = SB_WIDTH + SG_WIDTH + GLA_WIDTH
IN_SPLITS = [SB_WIDTH, SB_WIDTH, SB_WIDTH,
             SG_WIDTH, SG_WIDTH,
             GLA_KEY_WIDTH, GLA_KEY_WIDTH,
             GLA_WIDTH, GLA_WIDTH,
             GLA_GATE_RANK]
IN_WIDTH = sum(IN_SPLITS)

X_HEADS = 4
X_HEAD_DIM = D_MODEL // X_HEADS

PEER_HEADS = 8
PEER_KEYS = 128
PEER_N = PEER_KEYS * PEER_KEYS
PEER_DQ = 128
PEER_TOPK = 16
PEER_TOKENS = 128

kernel_name = 'hybrid_sb_sgu_gla_peer_block'


def rmsnorm(x, gain):
    xf = x.astype(jnp.float32)
    y = xf * lax.rsqrt(jnp.mean(xf * xf, axis=-1, keepdims=True) + EPS)
    return (y * gain.astype(jnp.float32)).astype(x.dtype)


def stick_breaking_attention(q, k, v):
    S = q.shape[2]
    scale = SB_HEAD_DIM ** -0.5
    outs = []
    for start in range(0, S, SB_BLOCK):
        end = start + SB_BLOCK
        kb, vb = k[:, :, :end], v[:, :, :end]
        z = jnp.einsum('bhtd,bhsd->bhts', q[:, :, start:end], kb).astype(jnp.float32) * scale
        t_pos = start + jnp.arange(SB_BLOCK)[:, None]
        s_pos = jnp.arange(end)[None, :]
        mask = s_pos < t_pos
        log_not = jnp.where(mask, jax.nn.log_sigmoid(-z), 0.0)
        later = lax.cumsum(log_not, axis=3, reverse=True) - log_not
        w = jnp.where(mask, jnp.exp(jax.nn.log_sigmoid(z) + later), 0.0)
        outs.append(jnp.einsum('bhts,bhsd->bhtd', w.astype(vb.dtype), vb))
    return jnp.concatenate(outs, axis=2)


def spatial_gating(u, v, v_gain, w_s, b_s):
    B, S, _ = u.shape
    u = jax.nn.gelu(u)
    v = rmsnorm(jax.nn.gelu(v), v_gain)
    v = v.reshape(B, S // SG_CHUNK, SG_CHUNK, SG_GROUPS, SG_GROUP_DIM)
    causal = jnp.tril(jnp.ones((SG_CHUNK, SG_CHUNK), dtype=bool))
    w = jnp.where(causal, w_s, jnp.zeros_like(w_s))
    mixed = jnp.einsum('gts,bcsgd->bctgd', w, v) + b_s.T[:, :, None]
    return u * mixed.reshape(B, S, SG_WIDTH)


def gated_linear_attention(q, k, v, log_a):
    B, H, S, dk = q.shape
    dv = v.shape[-1]
    n = S // GLA_CHUNK

    def chunks(t):
        t = t.astype(jnp.float32).reshape(B, H, n, GLA_CHUNK, t.shape[-1])
        return jnp.moveaxis(t, 2, 0)

    qc = chunks(q) * dk ** -0.5
    kc, vc, gc = chunks(k), chunks(v), chunks(log_a)
    causal = jnp.tril(jnp.ones((GLA_CHUNK, GLA_CHUNK), dtype=bool))[:, :, None]

    def step(state, inp):
        qi, ki, vi, gi = inp
        b = jnp.cumsum(gi, axis=2)
        o_inter = jnp.einsum('bhtk,bhkv->bhtv', qi * jnp.exp(b), state)
        diff = b[:, :, :, None, :] - b[:, :, None, :, :]
        decay = jnp.exp(jnp.where(causal, diff, -jnp.inf))
        scores = jnp.einsum('bhtk,bhsk,bhtsk->bhts', qi, ki, decay)
        o_intra = jnp.einsum('bhts,bhsv->bhtv', scores, vi)
        b_end = b[:, :, -1:, :]
        state = (jnp.exp(b_end[:, :, 0, :, None]) * state
                 + jnp.einsum('bhsk,bhsv->bhkv', ki * jnp.exp(b_end - b), vi))
        return state, o_inter + o_intra

    state0 = jnp.zeros((B, H, dk, dv), jnp.float32)
    _, o = lax.scan(step, state0, (qc, kc, vc, gc))
    o = jnp.moveaxis(o, 0, 2).reshape(B, H, S, dv)
    return o.astype(v.dtype)


def hybrid_mixer(h, w_in, sg_v_gain, sg_w_spatial, sg_b_spatial,
                 gla_w_gate, gla_b_gate, gla_out_gain, w_out):
    B, S, _ = h.shape
    proj = h @ w_in
    offsets = np.cumsum(IN_SPLITS)[:-1].tolist()
    sb_q, sb_k, sb_v, sg_u, sg_v, g_q, g_k, g_v, g_o, g_a = jnp.split(proj, offsets, axis=-1)

    def heads(t, n_heads):
        return t.reshape(B, S, n_heads, -1).transpose(0, 2, 1, 3)

    y_sb = stick_breaking_attention(heads(sb_q, SB_HEADS), heads(sb_k, SB_HEADS), heads(sb_v, SB_HEADS))
    y_sb = y_sb.transpose(0, 2, 1, 3).reshape(B, S, SB_WIDTH)

    y_sg = spatial_gating(sg_u, sg_v, sg_v_gain, sg_w_spatial, sg_b_spatial)

    log_a = jax.nn.log_sigmoid((g_a @ gla_w_gate + gla_b_gate).astype(jnp.float32)) / GLA_TAU
    y_gla = gated_linear_attention(heads(g_q, GLA_HEADS), heads(g_k, GLA_HEADS),
                                   heads(g_v, GLA_HEADS), heads(log_a, GLA_HEADS))
    y_gla = y_gla.transpose(0, 2, 1, 3)
    y_gla = rmsnorm(y_gla, gla_out_gain.reshape(GLA_HEADS, GLA_DV)).reshape(B, S, GLA_WIDTH)
    y_gla = y_gla * jax.nn.silu(g_o)

    return jnp.concatenate([y_sb, y_sg, y_gla], axis=-1) @ w_out


def memory_cross_attention(h, mem, mem_gain, w_q, w_kv, w_o):
    B, S, D = h.shape
    m = rmsnorm(mem, mem_gain)
    q = (h @ w_q).reshape(B, S, X_HEADS, X_HEAD_DIM)
    k, v = jnp.split(m @ w_kv, 2, axis=-1)
    k = k.reshape(B, -1, X_HEADS, X_HEAD_DIM)
    v = v.reshape(B, -1, X_HEADS, X_HEAD_DIM)
    s = jnp.einsum('bthd,bmhd->bhtm', q, k).astype(jnp.float32) * X_HEAD_DIM ** -0.5
    p = jax.nn.softmax(s, axis=-1).astype(v.dtype)
    o = jnp.einsum('bhtm,bmhd->bthd', p, v).reshape(B, S, D)
    return o @ w_o


def peer_ffn(h, w_q, sub_keys, expert_u, expert_v):
    B, S, D = h.shape
    tokens = h.reshape(-1, PEER_TOKENS, D)

    def block(xt):
        T = xt.shape[0]
        q = (xt @ w_q).reshape(T, PEER_HEADS, 2, PEER_DQ // 2)
        scores = jnp.einsum('thpd,hpnd->thpn', q, sub_keys).astype(jnp.float32)
        s_top, i_top = lax.top_k(scores, PEER_TOPK)
        cand = s_top[:, :, 0, :, None] + s_top[:, :, 1, None, :]
        cand_idx = i_top[:, :, 0, :, None] * PEER_KEYS + i_top[:, :, 1, None, :]
        c_s, c_i = lax.top_k(cand.reshape(T, PEER_HEADS, -1), PEER_TOPK)
        idx = jnp.take_along_axis(cand_idx.reshape(T, PEER_HEADS, -1), c_i, axis=-1)
        g = jax.nn.softmax(c_s, axis=-1).astype(xt.dtype)
        act = jax.nn.gelu(jnp.einsum('td,thkd->thk', xt, expert_u[idx]))
        return jnp.einsum('thk,thkd->td', g * act, expert_v[idx])

    return lax.map(block, tokens).reshape(B, S, D)


def setup_inputs(seed: int = 0) -> dict:
    key = jax.random.key(seed)
    ks = jax.random.split(key, 22)
    L, D = DEPTH, D_MODEL

    def nrm(k, shape, scale):
        return jax.random.normal(k, shape, jnp.float32) * scale

    def gain(k, shape):
        return 1.0 + 0.05 * jax.random.normal(k, shape, jnp.float32)

    return {
        'x': nrm(ks[0], (BATCH, SEQ, D), 1.0),
        'mem': nrm(ks[1], (BATCH, N_MEM, D), 1.0),
        'norm_mix': gain(ks[2], (L, D)),
        'w_in': nrm(ks[3], (L, D, IN_WIDTH), D ** -0.5),
        'sg_v_gain': gain(ks[4], (L, SG_WIDTH)),
        'sg_w_spatial': nrm(ks[5], (L, SG_GROUPS, SG_CHUNK, SG_CHUNK), SG_CHUNK ** -0.5),
        'sg_b_spatial': gain(ks[6], (L, SG_GROUPS, SG_CHUNK)),
        'gla_w_gate': nrm(ks[7], (L, GLA_GATE_RANK, GLA_KEY_WIDTH), GLA_GATE_RANK ** -0.5),
        'gla_b_gate': gain(ks[8], (L, GLA_KEY_WIDTH)),
        'gla_out_gain': gain(ks[9], (L, GLA_WIDTH)),
        'w_out': nrm(ks[10], (L, MIX_WIDTH, D), MIX_WIDTH ** -0.5),
        'norm_mem': gain(ks[11], (L, D)),
        'mem_gain': gain(ks[12], (L, D)),
        'w_cq': nrm(ks[13], (L, D, D), D ** -0.5),
        'w_ckv': nrm(ks[14], (L, D, 2 * D), D ** -0.5),
        'w_co': nrm(ks[15], (L, D, D), D ** -0.5),
        'norm_ffn': gain(ks[16], (L, D)),
        'peer_w_q': nrm(ks[17], (L, D, PEER_HEADS * PEER_DQ), D ** -0.5),
        'peer_sub_keys': nrm(ks[18], (L, PEER_HEADS, 2, PEER_KEYS, PEER_DQ // 2), (PEER_DQ // 2) ** -0.5),
        'peer_u': nrm(ks[19], (L, PEER_N, D), D ** -0.5),
        'peer_v': nrm(ks[20], (L, PEER_N, D), (PEER_HEADS * PEER_TOPK) ** -0.5),
        'final_gain': gain(ks[21], (D,)),
    }


def reference(x, mem, norm_mix, w_in, sg_v_gain, sg_w_spatial, sg_b_spatial,
              gla_w_gate, gla_b_gate, gla_out_gain, w_out, norm_mem, mem_gain,
              w_cq, w_ckv, w_co, norm_ffn, peer_w_q, peer_sub_keys, peer_u, peer_v,
              final_gain):
    for l in range(DEPTH):
        h = rmsnorm(x, norm_mix[l])
        x = x + hybrid_mixer(h, w_in[l], sg_v_gain[l], sg_w_spatial[l], sg_b_spatial[l],
                             gla_w_gate[l], gla_b_gate[l], gla_out_gain[l], w_out[l])
        h = rmsnorm(x, norm_mem[l])
        x = x + memory_cross_attention(h, mem, mem_gain[l], w_cq[l], w_ckv[l], w_co[l])
        h = rmsnorm(x, norm_ffn[l])
        x = x + peer_ffn(h, peer_w_q[l], peer_sub_keys[l], peer_u[l], peer_v[l])
    return rmsnorm(x, final_gain)
```

```python
import contextlib
import numpy as np
import concourse.bass as bass
import concourse.mybir as mybir
from concourse.bass_utils import run_bass_kernel_spmd

F32 = mybir.dt.float32
BF16 = mybir.dt.bfloat16
AF = mybir.ActivationFunctionType
ALU = mybir.AluOpType
AX = mybir.AxisListType

P = 128
S = 2048
D = 1024
NT = S // P
DEPTH = 2
NMEM = 256
NEXP = 16384
EPS = 1e-6
NEG = -1.0e30


class Tok:
    __slots__ = ("w", "r")

    def __init__(self):
        self.w = None
        self.r = {}


class Eng:
    def __init__(self, kb, h, name):
        self.h = h
        self.name = name
        self.sid = kb.new_sem(name)
        self.cnt = 0
        self.seen = {}


class KB:
    def __init__(self, nc):
        self.nc = nc
        self.es = contextlib.ExitStack()
        self.sems = []
        self.PE = Eng(self, nc.tensor, "pe")
        self.ACT = Eng(self, nc.scalar, "act")
        self.DVE = Eng(self, nc.vector, "dve")
        self.POOL = Eng(self, nc.gpsimd, "pool")
        self.SP = Eng(self, nc.sync, "sp")
        self.engs = [self.PE, self.ACT, self.DVE, self.POOL, self.SP]
        self.dsem = [self.new_sem("dma%d" % i) for i in range(28)]
        self.dtgt = {s: 0 for s in self.dsem}
        self.dnext = 0
        self.n_ins = 0

    def new_sem(self, name):
        s = self.es.enter_context(self.nc.semaphore(name))
        self.sems.append(s)
        return len(self.sems) - 1

    def wait(self, eng, sid, val):
        if val <= 0 or eng.seen.get(sid, 0) >= val:
            return
        eng.h.wait_ge(self.sems[sid], val)
        eng.seen[sid] = val

    def _deps(self, eng, R, W):
        deps = {}
        for t in R:
            if t.w is not None:
                s, v = t.w
                if deps.get(s, 0) < v:
                    deps[s] = v
        for t in W:
            if t.w is not None:
                s, v = t.w
                if deps.get(s, 0) < v:
                    deps[s] = v
            for s, v in t.r.items():
                if deps.get(s, 0) < v:
                    deps[s] = v
        for s, v in deps.items():
            if eng is self.PE and s == self.PE.sid:
                continue
            self.wait(eng, s, v)

    def _mark(self, me, R, W):
        s, v = me
        for t in W:
            t.w = me
            t.r = {}
        for t in R:
            if t.r.get(s, 0) < v:
                t.r[s] = v

    def op(self, eng, fn, R=(), W=()):
        self._deps(eng, R, W)
        ins = fn(eng.h)
        eng.cnt += 1
        ins.then_inc(self.sems[eng.sid], 1)
        self._mark((eng.sid, eng.cnt), R, W)
        self.n_ins += 1

    def dma(self, eng, out, in_, R=(), W=(), **kw):
        self._deps(eng, R, W)
        sid = self.dsem[self.dnext % len(self.dsem)]
        self.dnext += 1
        self.wait(eng, sid, self.dtgt[sid])
        self.dtgt[sid] += 16
        eng.h.dma_start(out=out, in_=in_, **kw).then_inc(self.sems[sid], 16)
        self._mark((sid, self.dtgt[sid]), R, W)
        self.n_ins += 1

    def barrier(self):
        for e in self.engs:
            for f in self.engs:
                if f is not e:
                    self.wait(e, f.sid, f.cnt)
            for sid in self.dsem:
                self.wait(e, sid, self.dtgt[sid])


def toks(n):
    return [Tok() for _ in range(n)]


def build(nseq=2, depth=DEPTH, dbg=None):
    nc = bass.Bass("TRN2", target_bir_lowering=False)
    kb = KB(nc)
    es = kb.es
    PE, ACT, DVE, POOL, SP = kb.PE, kb.ACT, kb.DVE, kb.POOL, kb.SP

    def din(name, shape):
        return nc.dram_tensor(name, list(shape), F32, kind="ExternalInput").ap()

    x_d = din("x", [nseq, S, D])
    mem_d = din("mem", [nseq, NMEM, D])
    norm_mix_d = din("norm_mix", [DEPTH, D])
    w_in_d = din("w_in", [DEPTH, D, 2832])
    sg_v_gain_d = din("sg_v_gain", [DEPTH, 256])
    sg_w_d = din("sg_w_spatial", [DEPTH, 4, P, P])
    sg_b_d = din("sg_b_spatial", [DEPTH, 4, P])
    gla_wg_d = din("gla_w_gate", [DEPTH, 16, P])
    gla_bg_d = din("gla_b_gate", [DEPTH, P])
    gla_og_d = din("gla_out_gain", [DEPTH, 256])
    w_out_d = din("w_out", [DEPTH, D, D])
    norm_mem_d = din("norm_mem", [DEPTH, D])
    mem_gain_d = din("mem_gain", [DEPTH, D])
    w_cq_d = din("w_cq", [DEPTH, D, D])
    w_ckv_d = din("w_ckv", [DEPTH, D, 2 * D])
    w_co_d = din("w_co", [DEPTH, D, D])
    norm_ffn_d = din("norm_ffn", [DEPTH, D])
    peer_wq_d = din("peer_w_q", [DEPTH, D, D])
    peer_sk_d = din("peer_sub_keys", [DEPTH, 8, 2, P, 64])
    peer_u_d = din("peer_u", [DEPTH, NEXP, D])
    peer_v_d = din("peer_v", [DEPTH, NEXP, D])
    final_gain_d = din("final_gain", [D])
    ident_d = din("c_ident", [P, P])
    mlt_d = din("c_mlt", [P, P])
    mle_d = din("c_mle", [P, P])
    mge_d = din("c_mge", [P, P])
    hm_d = din("c_hm", [P, 4])
    out_d = nc.dram_tensor("out", [nseq, S, D], F32, kind="ExternalOutput").ap()
    scratch_kind = "ExternalOutput" if (dbg or {}).get("dump_prep") else "Internal"
    ut_d = nc.dram_tensor("ut_bf", [DEPTH, D, NEXP], BF16, kind=scratch_kind).ap()
    vb_d = nc.dram_tensor("v_bf", [DEPTH, NEXP, D], BF16, kind=scratch_kind).ap()
    PEER_TILES = (dbg or {}).get("peer_tiles", NT)
    PEER_UPTO = (dbg or {}).get("peer_upto", 9)

    uid = [0]

    def sb(stack, name, shape, dt):
        uid[0] += 1
        return stack.enter_context(nc.sbuf_tensor("%s_%d" % (name, uid[0]), list(shape), dt))

    def pst(stack, name, shape, dt):
        uid[0] += 1
        return stack.enter_context(nc.psum_tensor("%s_%d" % (name, uid[0]), list(shape), dt))

    resid = sb(es, "resid", [P, NT, D], F32)
    r_tok = toks(NT)
    ident_f = sb(es, "ident_f", [P, P], F32)
    ident_b = sb(es, "ident_b", [P, P], BF16)
    mlt_f = sb(es, "mlt_f", [P, P], F32)
    mlt_b = sb(es, "mlt_b", [P, P], BF16)
    mle_f = sb(es, "mle_f", [P, P], F32)
    mge_f = sb(es, "mge_f", [P, P], F32)
    ones_f = sb(es, "ones_f", [P, P], F32)
    hm = sb(es, "hm", [P, 4], F32)
    gains = sb(es, "gains", [P, 8], F32)
    rstd = sb(es, "rstd", [P, NT], F32)
    ssq = sb(es, "ssq", [P, NT], F32)
    c_tok = Tok()
    g_tok = Tok()
    n_tok = Tok()

    kb.dma(SP, ident_f[:], ident_d[:, :], W=[c_tok])
    kb.dma(SP, mlt_f[:], mlt_d[:, :], W=[c_tok])
    kb.dma(SP, mle_f[:], mle_d[:, :], W=[c_tok])
    kb.dma(SP, mge_f[:], mge_d[:, :], W=[c_tok])
    kb.dma(SP, hm[:], hm_d[:, :], W=[c_tok])
    kb.dma(POOL, ident_b[:], ident_d[:, :], W=[c_tok])
    kb.dma(POOL, mlt_b[:], mlt_d[:, :], W=[c_tok])
    kb.op(DVE, lambda h: h.memset(ones_f[:], 1.0), W=[c_tok])

    with contextlib.ExitStack() as pes:
        pass

    def load_gain(vec_ap):
        with nc.allow_non_contiguous_dma(reason="tiny gain load"):
            kb.dma(SP, gains[:], vec_ap.rearrange("(c p) -> p c", p=P), W=[g_tok])

    def norm_tile(src_ap, src_tok, tt_idx, xn, xn_tok, pt_b, pt_tok, dst_ap, dst_tok, junk, junk_tok):
        col = rstd[:, tt_idx:tt_idx + 1]
        kb.op(ACT, lambda h: h.activation(out=junk[:], in_=src_ap, func=AF.Square, accum_out=ssq[:, tt_idx:tt_idx + 1]),
              R=[src_tok], W=[junk_tok, n_tok])
        kb.op(DVE, lambda h: h.tensor_scalar(out=col, in0=ssq[:, tt_idx:tt_idx + 1], scalar1=1.0 / D, scalar2=EPS,
                                             op0=ALU.mult, op1=ALU.add), R=[n_tok], W=[n_tok])
        kb.op(ACT, lambda h: h.activation(out=col, in_=col, func=AF.Sqrt), R=[n_tok], W=[n_tok])
        kb.op(DVE, lambda h: h.reciprocal(out=col, in_=col), R=[n_tok], W=[n_tok])
        kb.op(DVE, lambda h: h.tensor_scalar(out=xn[:], in0=src_ap, scalar1=col, scalar2=None, op0=ALU.mult),
              R=[src_tok, n_tok], W=[xn_tok])
        for c in range(8):
            kb.op(PE, lambda h, c=c: h.transpose(out=pt_b[:, c * P:(c + 1) * P], in_=xn[:, c * P:(c + 1) * P],
                                                 identity=ident_b[:]), R=[xn_tok, c_tok], W=[pt_tok])
        kb.op(DVE, lambda h: h.tensor_tensor(out=dst_ap, in0=pt_b[:].rearrange("p (c t) -> p c t", c=8),
                                             in1=gains[:, :, None].to_broadcast([P, 8, P]), op=ALU.mult),
              R=[pt_tok, g_tok], W=[dst_tok])

    def prep_peer_weights():
        with contextlib.ExitStack() as ps_:
            ub = [sb(ps_, "pp_ub%d" % i, [P, D], BF16) for i in range(2)]
            ub_t = toks(2)
            stg = [sb(ps_, "pp_stg%d" % i, [P, 8, 512], BF16) for i in range(2)]
            stg_t = toks(2)
            ptb = [pst(ps_, "pp_pt%d" % i, [P, 1024], BF16) for i in range(2)]
            ptb_t = toks(2)
            vst = [sb(ps_, "pp_vst%d" % i, [P, 4, D], BF16) for i in range(2)]
            vst_t = toks(2)
            for l in range(depth):
                for r in range((dbg or {}).get('prep_nv', NEXP // 512)):
                    i = r % 2
                    kb.dma(POOL, vst[i][:], peer_v_d[l, r * 512:(r + 1) * 512, :].rearrange("(c p) n -> p c n", p=P), W=[vst_t[i]])
                    kb.dma(SP, vb_d[l, r * 512:(r + 1) * 512, :].rearrange("(c p) n -> p c n", p=P), vst[i][:], R=[vst_t[i]])
                for et in range((dbg or {}).get('prep_nu', NEXP // P)):
                    i = et % 2
                    g = (et // 4) % 2
                    kb.dma(POOL, ub[i][:], peer_u_d[l, et * P:(et + 1) * P, :], W=[ub_t[i]])
                    for c in range(8):
                        kb.op(PE, lambda h, c=c, i=i: h.transpose(out=ptb[i][:, c * P:(c + 1) * P],
                                                                 in_=ub[i][:, c * P:(c + 1) * P], identity=ident_b[:]),
                              R=[ub_t[i], c_tok], W=[ptb_t[i]])
                    e2 = ACT if et % 2 == 0 else DVE
                    dst = stg[g][:, :, (et % 4) * P:(et % 4 + 1) * P]
                    src = ptb[i][:].rearrange("p (c e) -> p c e", c=8)
                    if e2 is ACT:
                        kb.op(ACT, lambda h, dst=dst, src=src: h.copy(out=dst, in_=src), R=[ptb_t[i]], W=[stg_t[g]])
                    else:
                        kb.op(DVE, lambda h, dst=dst, src=src: h.tensor_copy(out=dst, in_=src), R=[ptb_t[i]], W=[stg_t[g]])
                    if et % 4 == 3:
                        eg = et // 4
                        kb.dma(SP, ut_d[l, :, eg * 512:(eg + 1) * 512].rearrange("(c p) e -> p c e", p=P), stg[g][:],
                               R=[stg_t[g]])
        kb.barrier()

    def mixer(l, hT, h_tok):
        with contextlib.ExitStack() as ms:
            qkT = sb(ms, "qkT", [P, 8, S], BF16)
            qk_tok = toks(8)
            v_sb = sb(ms, "v_sb", [P, NT, 512], BF16)
            v_tok = toks(NT)
            with contextlib.ExitStack() as m1:
                wA = sb(m1, "wA", [P, 8, 1024], BF16)
                wA_tok = Tok()
                kb.dma(POOL, wA[:], w_in_d[l, :, 0:1024].rearrange("(c p) n -> p c n", p=P), W=[wA_tok])
                psq = [pst(m1, "psq%d" % i, [P, 512], F32) for i in range(2)]
                psq_t = toks(2)
                n = 0
                for cc in range(8):
                    for tg in range(4):
                        i = n % 2
                        n += 1
                        for k in range(8):
                            kb.op(PE, lambda h, k=k, i=i, cc=cc, tg=tg: h.matmul(
                                psq[i][:], lhsT=wA[:, k, cc * P:(cc + 1) * P], rhs=hT[:, k, tg * 512:(tg + 1) * 512],
                                start=(k == 0), stop=(k == 7)),
                                R=[wA_tok] + h_tok[tg * 4:tg * 4 + 4], W=[psq_t[i]])
                        sc = 0.125 if cc < 4 else 1.0
                        dst = qkT[:, cc, tg * 512:(tg + 1) * 512]
                        if n % 2 == 0:
                            kb.op(ACT, lambda h, dst=dst, i=i, sc=sc: h.mul(out=dst, in_=psq[i][:], mul=sc),
                                  R=[psq_t[i]], W=[qk_tok[cc]])
                        else:
                            kb.op(DVE, lambda h, dst=dst, i=i, sc=sc: h.tensor_scalar(out=dst, in0=psq[i][:], scalar1=sc,
                                                                                    scalar2=None, op0=ALU.mult),
                                  R=[psq_t[i]], W=[qk_tok[cc]])
            kb.barrier()
            with contextlib.ExitStack() as m2:
                wB = sb(m2, "wB", [P, 8, 1536], BF16)
                wC = sb(m2, "wC", [P, 8, 272], BF16)
                w_tok = Tok()
                kb.dma(POOL, wB[:, :, 0:1024], w_in_d[l, :, 1024:2048].rearrange("(c p) n -> p c n", p=P), W=[w_tok])
                kb.dma(POOL, wB[:, :, 1024:1536], w_in_d[l, :, 2304:2816].rearrange("(c p) n -> p c n", p=P), W=[w_tok])
                kb.dma(POOL, wC[:, :, 0:256], w_in_d[l, :, 2048:2304].rearrange("(c p) n -> p c n", p=P), W=[w_tok])
                with nc.allow_non_contiguous_dma(reason="small"):
                    kb.dma(POOL, wC[:, :, 256:272], w_in_d[l, :, 2816:2832].rearrange("(c p) n -> p c n", p=P), W=[w_tok])
                sgg = sb(m2, "sgg", [P, 256], F32)
                glg = sb(m2, "glg", [P, 256], F32)
                bsT = sb(m2, "bsT", [P, 4], F32)
                nbg = sb(m2, "nbg", [P, 1], F32)
                wgt = sb(m2, "wgt", [16, P], BF16)
                wsn = sb(m2, "wsn", [P, 4, P], F32)
                wsT = sb(m2, "wsT", [P, 4, P], BF16)
                sp_tok = Tok()
                with nc.allow_non_contiguous_dma(reason="small params"):
                    kb.dma(SP, sgg[:], sg_v_gain_d[l, :].partition_broadcast(P), W=[sp_tok])
                    kb.dma(SP, glg[:], gla_og_d[l, :].partition_broadcast(P), W=[sp_tok])
                    kb.dma(SP, bsT[:], sg_b_d[l].rearrange("g t -> t g"), W=[sp_tok])
                    kb.dma(SP, nbg[:], gla_bg_d[l, :].rearrange("(p o) -> p o", o=1), W=[sp_tok])
                    kb.dma(POOL, wgt[:], gla_wg_d[l], W=[sp_tok])
                    kb.dma(SP, wsn[:], sg_w_d[l].rearrange("g t s -> t g s"), W=[sp_tok])
                kb.op(DVE, lambda h: h.tensor_scalar(out=nbg[:], in0=nbg[:], scalar1=-1.0, scalar2=None, op0=ALU.mult),
                      R=[sp_tok], W=[sp_tok])
                kb.op(DVE, lambda h: h.tensor_tensor(out=wsn[:], in0=wsn[:], in1=mle_f[:, None, :].to_broadcast([P, 4, P]),
                                                     op=ALU.mult), R=[sp_tok, c_tok], W=[sp_tok])
                ps_tm = [pst(m2, "ps_tm%d" % i, [P, 512], F32) for i in range(3)]
                ps_tm_t = toks(3)
                ps_f = pst(m2, "ps_f", [P, 512], F32)
                ps_f_t = Tok()
                ps_g = pst(m2, "ps_g", [P, 512], F32)
                ps_g_t = Tok()
                ps_s = pst(m2, "ps_s", [P, 512], F32)
                ps_s_t = Tok()
                ps_b = pst(m2, "ps_b", [P, 1024], BF16)
                ps_b_t = Tok()
                ps_st = pst(m2, "ps_st", [P, 512], F32)
                ps_st_t = Tok()
                for g in range(4):
                    kb.op(PE, lambda h, g=g: h.transpose(out=ps_f[:, g * P:(g + 1) * P], in_=wsn[:, g, :], identity=ident_f[:]),
                          R=[sp_tok, c_tok], W=[ps_f_t])
                kb.op(ACT, lambda h: h.copy(out=wsT[:].rearrange("p g t -> p (g t)"), in_=ps_f[:]), R=[ps_f_t], W=[sp_tok])
                gu = sb(m2, "gu", [P, 256], F32)
                gv = sb(m2, "gv", [P, 256], F32)
                junk = sb(m2, "junk2", [P, 256], F32)
                vn = sb(m2, "vn", [P, 256], BF16)
                ysg = sb(m2, "ysg", [P, 256], BF16)
                st4 = sb(m2, "st4", [P, 8], F32)
                gvt = sb(m2, "gvt", [P, 256], BF16)
                gsl = sb(m2, "gsl", [P, 256], F32)
                gaT = sb(m2, "gaT", [16, P], BF16)
                e1 = sb(m2, "e1", [P, P], F32)
                cs = sb(m2, "cs", [P, P], F32)
                ebt = sb(m2, "ebt", [P, P], F32)
                eit = sb(m2, "eit", [P, P], F32)
                qeT = sb(m2, "qeT", [P, P], BF16)
                keT = sb(m2, "keT", [P, P], BF16)
                ke_tm = sb(m2, "ke_tm", [P, P], BF16)
                scT = sb(m2, "scT", [P, 4, P], BF16)
                state = sb(m2, "state", [P, 64], F32)
                state_b = sb(m2, "state_b", [P, 4, 64], BF16)
                keT4 = sb(m2, "keT4", [P, 4, P], BF16)
                tmp4 = sb(m2, "tmp4", [P, 4, 64], F32)
                stmp = sb(m2, "stmp", [P, 64], F32)
                o_sb = sb(m2, "o_sb", [P, 256], F32)
                osq = sb(m2, "osq", [P, 256], F32)
                ygl = sb(m2, "ygl", [P, 256], BF16)
                T = {k_: Tok() for k_ in ["gu", "gv", "junk", "vn", "ysg", "st4", "gvt", "gsl", "gaT", "e1", "cs", "ebt",
                                          "eit", "qeT", "keT", "ke_tm", "scT", "state", "state_b", "stmp", "o_sb", "osq",
                                          "ygl", "keT4", "tmp4"]}
                kb.op(DVE, lambda h: h.memset(state[:], 0.0), W=[T["state"]])
                kb.op(DVE, lambda h: h.memset(state_b[:], 0.0), W=[T["state_b"]])
                for tt in range(NT):
                    tok = slice(tt * P, (tt + 1) * P)
                    ht = h_tok[tt]
                    for g in range(3):
                        for k in range(8):
                            kb.op(PE, lambda h, g=g, k=k: h.matmul(ps_tm[g][:], lhsT=hT[:, k, tok],
                                                                   rhs=wB[:, k, g * 512:(g + 1) * 512],
                                                                   start=(k == 0), stop=(k == 7)),
                                  R=[ht, w_tok], W=[ps_tm_t[g]])
                    for j in range(2):
                        for k in range(8):
                            kb.op(PE, lambda h, j=j, k=k: h.matmul(ps_f[:, j * P:(j + 1) * P], lhsT=wC[:, k, j * P:(j + 1) * P],
                                                                   rhs=hT[:, k, tok], start=(k == 0), stop=(k == 7)),
                                  R=[ht, w_tok], W=[ps_f_t])
                    for k in range(8):
                        kb.op(PE, lambda h, k=k: h.matmul(ps_g[0:16, 0:P], lhsT=wC[:, k, 256:272], rhs=hT[:, k, tok],
                                                          start=(k == 0), stop=(k == 7)), R=[ht, w_tok], W=[ps_g_t])
                    kb.op(ACT, lambda h: h.copy(out=v_sb[:, tt, :], in_=ps_tm[0][:]), R=[ps_tm_t[0]], W=[v_tok[tt]])
                    kb.op(ACT, lambda h: h.activation(out=gu[:], in_=ps_tm[1][:, 0:256], func=AF.Gelu_apprx_tanh),
                          R=[ps_tm_t[1]], W=[T["gu"]])
                    kb.op(ACT, lambda h: h.activation(out=gv[:], in_=ps_tm[1][:, 256:512], func=AF.Gelu_apprx_tanh),
                          R=[ps_tm_t[1]], W=[T["gv"]])
                    kb.op(ACT, lambda h: h.activation(out=junk[:], in_=gv[:], func=AF.Square, accum_out=st4[:, 0:1]),
                          R=[T["gv"]], W=[T["junk"], T["st4"]])
                    kb.op(DVE, lambda h: h.tensor_scalar(out=st4[:, 0:1], in0=st4[:, 0:1], scalar1=1.0 / 256, scalar2=EPS,
                                                         op0=ALU.mult, op1=ALU.add), R=[T["st4"]], W=[T["st4"]])
                    kb.op(ACT, lambda h: h.activation(out=st4[:, 0:1], in_=st4[:, 0:1], func=AF.Sqrt), R=[T["st4"]], W=[T["st4"]])
                    kb.op(DVE, lambda h: h.reciprocal(out=st4[:, 0:1], in_=st4[:, 0:1]), R=[T["st4"]], W=[T["st4"]])
                    kb.op(DVE, lambda h: h.scalar_tensor_tensor(out=vn[:], in0=gv[:], scalar=st4[:, 0:1], in1=sgg[:],
                                                                op0=ALU.mult, op1=ALU.mult),
                          R=[T["gv"], T["st4"], sp_tok], W=[T["vn"]])
                    kb.op(ACT, lambda h: h.copy(out=gaT[:], in_=ps_g[0:16, 0:P]), R=[ps_g_t], W=[T["gaT"]])
                    for g in range(4):
                        kb.op(PE, lambda h, g=g: h.matmul(ps_g[:, 128 + g * 64:128 + (g + 1) * 64], lhsT=wsT[:, g, :],
                                                          rhs=vn[:, g * 64:(g + 1) * 64], start=True, stop=True),
                              R=[T["vn"], sp_tok], W=[ps_g_t])
                    for g in range(4):
                        kb.op(DVE, lambda h, g=g: h.scalar_tensor_tensor(
                            out=ysg[:, g * 64:(g + 1) * 64], in0=ps_g[:, 128 + g * 64:128 + (g + 1) * 64],
                            scalar=bsT[:, g:g + 1], in1=gu[:, g * 64:(g + 1) * 64], op0=ALU.add, op1=ALU.mult),
                            R=[ps_g_t, T["gu"], sp_tok], W=[T["ysg"]])
                    kb.op(ACT, lambda h: h.copy(out=gvt[:], in_=ps_tm[2][:, 0:256]), R=[ps_tm_t[2]], W=[T["gvt"]])
                    kb.op(ACT, lambda h: h.activation(out=gsl[:], in_=ps_tm[2][:, 256:512], func=AF.Silu),
                          R=[ps_tm_t[2]], W=[T["gsl"]])
                    kb.op(DVE, lambda h: h.tensor_tensor(out=gsl[:], in0=gsl[:], in1=glg[:], op=ALU.mult),
                          R=[T["gsl"], sp_tok], W=[T["gsl"]])
                    kb.op(PE, lambda h: h.matmul(ps_f[:, 256:384], lhsT=wgt[:, :], rhs=gaT[:, :], start=True, stop=True),
                          R=[T["gaT"], sp_tok], W=[ps_f_t])
                    kb.op(ACT, lambda h: h.activation(out=e1[:], in_=ps_f[:, 256:384], func=AF.Exp, bias=nbg[:, 0:1], scale=-1.0),
                          R=[ps_f_t, sp_tok], W=[T["e1"]])
                    kb.op(ACT, lambda h: h.activation(out=e1[:], in_=e1[:], func=AF.Ln, bias=1.0), R=[T["e1"]], W=[T["e1"]])
                    kb.op(DVE, lambda h: h.tensor_tensor_scan(out=cs[:], data0=ones_f[:], data1=e1[:], initial=0.0,
                                                              op0=ALU.mult, op1=ALU.add), R=[T["e1"], c_tok], W=[T["cs"]])
                    kb.op(ACT, lambda h: h.activation(out=ebt[:], in_=cs[:], func=AF.Exp, scale=-1.0 / 16), R=[T["cs"]], W=[T["ebt"]])
                    kb.op(ACT, lambda h: h.activation(out=eit[:], in_=cs[:], func=AF.Exp, scale=1.0 / 16), R=[T["cs"]], W=[T["eit"]])
                    kb.op(DVE, lambda h: h.scalar_tensor_tensor(out=qeT[:], in0=ps_f[:, 0:P], scalar=32.0 ** -0.5, in1=ebt[:],
                                                                op0=ALU.mult, op1=ALU.mult), R=[ps_f_t, T["ebt"]], W=[T["qeT"]])
                    kb.op(DVE, lambda h: h.tensor_tensor(out=keT[:], in0=ps_f[:, P:2 * P], in1=eit[:], op=ALU.mult),
                          R=[ps_f_t, T["eit"]], W=[T["keT"]])
                    for hh in range(4):
                        kb.op(DVE, lambda h, hh=hh: h.scalar_tensor_tensor(out=keT4[:, hh, :], in0=ps_f[:, P:2 * P], scalar=hm[:, hh:hh + 1],
                                                                           in1=eit[:], op0=ALU.mult, op1=ALU.mult),
                              R=[ps_f_t, T["eit"], c_tok], W=[T["keT4"]])
                    kb.op(PE, lambda h: h.transpose(out=ps_b[:, 0:P], in_=keT[:], identity=ident_b[:]),
                          R=[T["keT"], c_tok], W=[ps_b_t])
                    kb.op(ACT, lambda h: h.copy(out=ke_tm[:], in_=ps_b[:, 0:P]), R=[ps_b_t], W=[T["ke_tm"]])
                    for hh in range(4):
                        kb.op(PE, lambda h, hh=hh: h.matmul(ps_s[:, hh * P:(hh + 1) * P], lhsT=keT4[:, hh, :],
                                                            rhs=qeT[:, :], start=True, stop=True),
                              R=[T["keT4"], T["qeT"]], W=[ps_s_t])
                    kb.op(DVE, lambda h: h.tensor_tensor(out=scT[:], in0=ps_s[:].rearrange("p (g t) -> p g t", g=4),
                                                         in1=mge_f[:, None, :].to_broadcast([P, 4, P]), op=ALU.mult),
                          R=[ps_s_t, c_tok], W=[T["scT"]])
                    for hh in range(4):
                        kb.op(PE, lambda h, hh=hh: h.matmul(ps_st[:, hh * 64:(hh + 1) * 64], lhsT=scT[:, hh, :],
                                                            rhs=gvt[:, hh * 64:(hh + 1) * 64], start=True, stop=False),
                              R=[T["scT"], T["gvt"]], W=[ps_st_t])
                        kb.op(PE, lambda h, hh=hh: h.matmul(ps_st[:, hh * 64:(hh + 1) * 64], lhsT=qeT[:, :],
                                                            rhs=state_b[:, hh, :], start=False, stop=True),
                              R=[T["qeT"], T["state_b"]], W=[ps_st_t])
                    kb.op(ACT, lambda h: h.copy(out=o_sb[:], in_=ps_st[:, 0:256]), R=[ps_st_t], W=[T["o_sb"]])
                    kb.op(PE, lambda h: h.matmul(ps_st[:, 256:512], lhsT=ke_tm[:, :], rhs=gvt[:, :], start=True, stop=True),
                          R=[T["ke_tm"], T["gvt"]], W=[ps_st_t])
                    kb.op(DVE, lambda h: h.tensor_tensor(out=tmp4[:], in0=ps_st[:, 256:512].rearrange("p (g v) -> p g v", g=4),
                                                         in1=hm[:, :, None].to_broadcast([P, 4, 64]), op=ALU.mult),
                          R=[ps_st_t, c_tok], W=[T["tmp4"]])
                    kb.op(DVE, lambda h: h.tensor_reduce(out=stmp[:], in_=tmp4[:].rearrange("p g v -> p v g"), axis=AX.X, op=ALU.add),
                          R=[T["tmp4"]], W=[T["stmp"]])
                    kb.op(DVE, lambda h: h.tensor_tensor(out=stmp[:], in0=stmp[:], in1=state[:], op=ALU.add),
                          R=[T["stmp"], T["state"]], W=[T["stmp"]])
                    kb.op(DVE, lambda h: h.tensor_scalar(out=state[:], in0=stmp[:], scalar1=ebt[:, P - 1:P], scalar2=None,
                                                         op0=ALU.mult), R=[T["stmp"], T["ebt"]], W=[T["state"]])
                    kb.op(DVE, lambda h: h.tensor_tensor(out=state_b[:], in0=state[:, None, :].to_broadcast([P, 4, 64]),
                                                         in1=hm[:, :, None].to_broadcast([P, 4, 64]), op=ALU.mult),
                          R=[T["state"], c_tok], W=[T["state_b"]])
                    kb.op(DVE, lambda h: h.tensor_tensor(out=osq[:], in0=o_sb[:], in1=o_sb[:], op=ALU.mult),
                          R=[T["o_sb"]], W=[T["osq"]])
                    kb.op(DVE, lambda h: h.tensor_reduce(out=st4[:, 4:8], in_=osq[:].rearrange("p (g v) -> p g v", g=4),
                                                         axis=AX.X, op=ALU.add), R=[T["osq"]], W=[T["st4"]])
                    kb.op(DVE, lambda h: h.tensor_scalar(out=st4[:, 4:8], in0=st4[:, 4:8], scalar1=1.0 / 64, scalar2=EPS,
                                                         op0=ALU.mult, op1=ALU.add), R=[T["st4"]], W=[T["st4"]])
                    kb.op(ACT, lambda h: h.activation(out=st4[:, 4:8], in_=st4[:, 4:8], func=AF.Sqrt), R=[T["st4"]], W=[T["st4"]])
                    kb.op(DVE, lambda h: h.reciprocal(out=st4[:, 4:8], in_=st4[:, 4:8]), R=[T["st4"]], W=[T["st4"]])
                    kb.op(DVE, lambda h: h.tensor_tensor(out=osq[:].rearrange("p (g v) -> p g v", g=4),
                                                         in0=o_sb[:].rearrange("p (g v) -> p g v", g=4),
                                                         in1=st4[:, 4:8, None].to_broadcast([P, 4, 64]), op=ALU.mult),
                          R=[T["o_sb"], T["st4"]], W=[T["osq"]])
                    kb.op(DVE, lambda h: h.tensor_tensor(out=ygl[:], in0=osq[:], in1=gsl[:], op=ALU.mult),
                          R=[T["osq"], T["gsl"]], W=[T["ygl"]])
                    for j in range(2):
                        kb.op(PE, lambda h, j=j: h.transpose(out=ps_b[:, (2 + j) * P:(3 + j) * P], in_=ysg[:, j * P:(j + 1) * P],
                                                             identity=ident_b[:]), R=[T["ysg"], c_tok], W=[ps_b_t])
                    for j in range(2):
                        kb.op(PE, lambda h, j=j: h.transpose(out=ps_b[:, (4 + j) * P:(5 + j) * P], in_=ygl[:, j * P:(j + 1) * P],
                                                             identity=ident_b[:]), R=[T["ygl"], c_tok], W=[ps_b_t])
                    kb.op(ACT, lambda h: h.copy(out=hT[:, 4:8, tok], in_=ps_b[:, 2 * P:6 * P].rearrange("p (c t) -> p c t", c=4)),
                          R=[ps_b_t], W=[ht])
            kb.barrier()
            with contextlib.ExitStack() as m3:
                psz = [pst(m3, "psz%d" % i, [P, 512], F32) for i in range(4)]
                psz_t = toks(4)
                pstr = [pst(m3, "pstr%d" % i, [P, 1024], BF16) for i in range(2)]
                pstr_t = toks(2)
                psy = [pst(m3, "psy%d" % i, [P, 512], F32) for i in range(2)]
                psy_t = toks(2)
                NB = 2
                A = [sb(m3, "sbA%d" % i, [P, S], F32) for i in range(NB)]
                B = [sb(m3, "sbB%d" % i, [P, S], F32) for i in range(NB)]
                Wb = [sb(m3, "sbW%d" % i, [P, S], BF16) for i in range(NB)]
                WT = [sb(m3, "sbWT%d" % i, [P, NT, P], BF16) for i in range(NB)]
                ntot = [sb(m3, "ntot%d" % i, [P, 1], F32) for i in range(NB)]
                A_t, B_t, Wb_t, WT_t, nt_t = toks(NB), toks(NB), toks(NB), toks(NB), toks(NB)
                def sb_params(n):
                    qb, hd = n // 8, n % 8
                    nk = qb + 1
                    Sk = nk * P
                    return qb, hd, nk, Sk, (Sk + 511) // 512, n % NB, hd // 2, 4 + hd // 2, (hd % 2) * 64

                def sb_stage1(n):
                    qb, hd, nk, Sk, nbk, u, cq, ck, hp = sb_params(n)
                    for j in range(nbk):
                        w = min(512, Sk - j * 512)
                        kb.op(PE, lambda h, j=j, w=w: h.matmul(psz[j][:, 0:w], lhsT=qkT[hp:hp + 64, cq, qb * P:(qb + 1) * P],
                                                               rhs=qkT[hp:hp + 64, ck, j * 512:j * 512 + w], start=True, stop=True),
                              R=[qk_tok[cq], qk_tok[ck]], W=[psz_t[j]])
                    for j in range(nbk):
                        w = min(512, Sk - j * 512)
                        kb.op(ACT, lambda h, j=j, w=w: h.activation(out=A[u][:, j * 512:j * 512 + w], in_=psz[j][:, 0:w], func=AF.Exp),
                              R=[psz_t[j]], W=[A_t[u]])
                    kb.op(ACT, lambda h: h.activation(out=A[u][:, 0:Sk], in_=A[u][:, 0:Sk], func=AF.Ln, bias=1.0),
                          R=[A_t[u]], W=[A_t[u]])
                    kb.op(POOL, lambda h: h.tensor_tensor(out=A[u][:, qb * P:Sk], in0=A[u][:, qb * P:Sk], in1=mlt_f[:], op=ALU.mult),
                          R=[A_t[u], c_tok], W=[A_t[u]])
                    kb.op(DVE, lambda h: h.tensor_tensor_scan(out=B[u][:, 0:Sk], data0=ones_f[:, 0:1].to_broadcast([P, Sk]),
                                                              data1=A[u][:, 0:Sk], initial=0.0, op0=ALU.mult, op1=ALU.add),
                          R=[A_t[u], c_tok], W=[B_t[u]])
                    kb.op(DVE, lambda h: h.tensor_scalar(out=ntot[u][:], in0=B[u][:, Sk - 1:Sk], scalar1=-1.0, scalar2=None,
                                                         op0=ALU.mult), R=[B_t[u]], W=[nt_t[u]])
                    kb.op(DVE, lambda h: h.tensor_copy(out=A[u][:, 0:1], in_=psz[0][:, 0:1]), R=[psz_t[0], A_t[u]], W=[A_t[u]])
                    for j in range(nbk):
                        w = min(512, Sk - j * 512)
                        lo = 1 if j == 0 else 0
                        kb.op(DVE, lambda h, j=j, w=w, lo=lo: h.tensor_tensor(out=A[u][:, j * 512 + lo:j * 512 + w], in0=psz[j][:, lo:w],
                                                                             in1=B[u][:, j * 512 + lo - 1:j * 512 + w - 1], op=ALU.add),
                              R=[psz_t[j], A_t[u], B_t[u]], W=[A_t[u]])

                def sb_stage2(n):
                    qb, hd, nk, Sk, nbk, u, cq, ck, hp = sb_params(n)
                    kb.op(ACT, lambda h: h.activation(out=Wb[u][:, 0:Sk], in_=A[u][:, 0:Sk], func=AF.Exp, bias=ntot[u][:, 0:1]),
                          R=[A_t[u], nt_t[u]], W=[Wb_t[u]])
                    kb.op(POOL, lambda h: h.tensor_tensor(out=Wb[u][:, qb * P:Sk], in0=Wb[u][:, qb * P:Sk], in1=mlt_b[:], op=ALU.mult),
                          R=[Wb_t[u], c_tok], W=[Wb_t[u]])
                    for g0 in range(0, nk, 8):
                        gi = (g0 // 8) % 2
                        ng = min(8, nk - g0)
                        for kk in range(ng):
                            kb.op(PE, lambda h, kk=kk, g0=g0, gi=gi: h.transpose(out=pstr[gi][:, kk * P:(kk + 1) * P],
                                                                                in_=Wb[u][:, (g0 + kk) * P:(g0 + kk + 1) * P],
                                                                                identity=ident_b[:]),
                                  R=[Wb_t[u], c_tok], W=[pstr_t[gi]])
                        dst = WT[u][:, g0:g0 + ng, :].rearrange("p k t -> p (k t)")
                        kb.op(ACT, lambda h, dst=dst, gi=gi, ng=ng: h.copy(out=dst, in_=pstr[gi][:, 0:ng * P]),
                              R=[pstr_t[gi]], W=[WT_t[u]])
                    yi = n % 2
                    for kk in range(nk):
                        kb.op(PE, lambda h, kk=kk, yi=yi: h.matmul(psy[yi][hp:hp + 64, 0:P], lhsT=v_sb[:, kk, hd * 64:(hd + 1) * 64],
                                                                   rhs=WT[u][:, kk, :], start=(kk == 0), stop=(kk == nk - 1)),
                              R=[v_tok[kk], WT_t[u]], W=[psy_t[yi]])
                    kb.op(ACT, lambda h, yi=yi: h.copy(out=hT[hp:hp + 64, cq, qb * P:(qb + 1) * P], in_=psy[yi][hp:hp + 64, 0:P]),
                          R=[psy_t[yi]], W=[h_tok[qb]])

                NU = NT * 8
                sb_stage1(0)
                for n in range(NU):
                    if n + 1 < NU:
                        sb_stage1(n + 1)
                    sb_stage2(n)
            kb.barrier()
        with contextlib.ExitStack() as m4:
            wo = sb(m4, "wo", [P, 8, D], BF16)
            wo_t = Tok()
            kb.dma(POOL, wo[:], w_out_d[l].rearrange("(c p) n -> p c n", p=P), W=[wo_t])
            pso = [pst(m4, "pso%d" % i, [P, 512], F32) for i in range(4)]
            pso_t = toks(4)
            n = 0
            for tt in range(NT):
                for ng in range(2):
                    i = n % 4
                    n += 1
                    for k in range(8):
                        kb.op(PE, lambda h, k=k, i=i, ng=ng, tt=tt: h.matmul(pso[i][:], lhsT=hT[:, k, tt * P:(tt + 1) * P],
                                                                             rhs=wo[:, k, ng * 512:(ng + 1) * 512],
                                                                             start=(k == 0), stop=(k == 7)),
                              R=[h_tok[tt], wo_t], W=[pso_t[i]])
                    dst = resid[:, tt, ng * 512:(ng + 1) * 512]
                    kb.op(DVE, lambda h, dst=dst, i=i: h.tensor_tensor(out=dst, in0=pso[i][:], in1=dst, op=ALU.add),
                          R=[pso_t[i], r_tok[tt]], W=[r_tok[tt]])
            kb.barrier()

    def norm_seq(gain_vec_ap, hT, h_tok):
        load_gain(gain_vec_ap)
        with contextlib.ExitStack() as ns:
            xn = [sb(ns, "ns_xn%d" % i, [P, D], BF16) for i in range(2)]
            xn_t = toks(2)
            junk = sb(ns, "ns_junk", [P, D], BF16)
            junk_t = Tok()
            ptb = [pst(ns, "ns_pt%d" % i, [P, 1024], BF16) for i in range(2)]
            ptb_t = toks(2)
            for tt in range(NT):
                i = tt % 2
                norm_tile(resid[:, tt, :], r_tok[tt], tt, xn[i], xn_t[i], ptb[i], ptb_t[i],
                          hT[:, :, tt * P:(tt + 1) * P], h_tok[tt], junk, junk_t)
        kb.barrier()

    def cross_attn(l, b, hT, h_tok):
        with contextlib.ExitStack() as cs_:
            kT = sb(cs_, "x_kT", [P, 8, NMEM], BF16)
            vx = sb(cs_, "x_v", [P, 2, D], BF16)
            qT = sb(cs_, "x_qT", [P, 8, S], BF16)
            kv_t, q_t = Tok(), toks(4)
            with contextlib.ExitStack() as c1:
                wkv = sb(c1, "x_wkv", [P, 8, 2 * D], BF16)
                wq = sb(c1, "x_wq", [P, 8, D], BF16)
                wkv_t, wq_t = Tok(), Tok()
                kb.dma(POOL, wkv[:], w_ckv_d[l].rearrange("(c p) n -> p c n", p=P), W=[wkv_t])
                kb.dma(POOL, wq[:], w_cq_d[l].rearrange("(c p) n -> p c n", p=P), W=[wq_t])
                mT = sb(c1, "x_mT", [P, 8, NMEM], BF16)
                mT_t = toks(2)
                mt = [sb(c1, "x_mt%d" % i, [P, D], F32) for i in range(2)]
                mt_t = toks(2)
                xn = [sb(c1, "x_xn%d" % i, [P, D], BF16) for i in range(2)]
                xn_t = toks(2)
                junk = sb(c1, "x_junk", [P, D], BF16)
                junk_t = Tok()
                ptb = [pst(c1, "x_pt%d" % i, [P, 1024], BF16) for i in range(2)]
                ptb_t = toks(2)
                psx = [pst(c1, "x_ps%d" % i, [P, 512], F32) for i in range(2)]
                psx_t = toks(2)
                load_gain(mem_gain_d[l, :])
                for i in range(2):
                    kb.dma(SP, mt[i][:], mem_d[b, i * P:(i + 1) * P, :], W=[mt_t[i]])
                    norm_tile(mt[i][:], mt_t[i], i, xn[i], xn_t[i], ptb[i], ptb_t[i], mT[:, :, i * P:(i + 1) * P], mT_t[i],
                              junk, junk_t)
                n = 0
                for cc in range(8):
                    i = n % 2
                    n += 1
                    for k in range(8):
                        kb.op(PE, lambda h, k=k, i=i, cc=cc: h.matmul(psx[i][:, 0:NMEM], lhsT=wkv[:, k, cc * P:(cc + 1) * P],
                                                                      rhs=mT[:, k, :], start=(k == 0), stop=(k == 7)),
                              R=[wkv_t] + mT_t, W=[psx_t[i]])
                    kb.op(ACT, lambda h, i=i, cc=cc: h.copy(out=kT[:, cc, :], in_=psx[i][:, 0:NMEM]), R=[psx_t[i]], W=[kv_t])
                for mi in range(2):
                    for ng in range(2):
                        i = n % 2
                        n += 1
                        for k in range(8):
                            kb.op(PE, lambda h, k=k, i=i, mi=mi, ng=ng: h.matmul(psx[i][:], lhsT=mT[:, k, mi * P:(mi + 1) * P],
                                                                                 rhs=wkv[:, k, D + ng * 512:D + (ng + 1) * 512],
                                                                                 start=(k == 0), stop=(k == 7)),
                                  R=[wkv_t] + mT_t, W=[psx_t[i]])
                        kb.op(DVE, lambda h, i=i, mi=mi, ng=ng: h.tensor_copy(out=vx[:, mi, ng * 512:(ng + 1) * 512], in_=psx[i][:]),
                              R=[psx_t[i]], W=[kv_t])
                for cc in range(8):
                    for tg in range(4):
                        i = n % 2
                        n += 1
                        for k in range(8):
                            kb.op(PE, lambda h, k=k, i=i, cc=cc, tg=tg: h.matmul(psx[i][:], lhsT=wq[:, k, cc * P:(cc + 1) * P],
                                                                                 rhs=hT[:, k, tg * 512:(tg + 1) * 512],
                                                                                 start=(k == 0), stop=(k == 7)),
                                  R=[wq_t] + h_tok[tg * 4:tg * 4 + 4], W=[psx_t[i]])
                        dst = qT[:, cc, tg * 512:(tg + 1) * 512]
                        if n % 2 == 0:
                            kb.op(ACT, lambda h, dst=dst, i=i: h.mul(out=dst, in_=psx[i][:], mul=1.0 / 16), R=[psx_t[i]], W=[q_t[tg]])
                        else:
                            kb.op(DVE, lambda h, dst=dst, i=i: h.tensor_scalar(out=dst, in0=psx[i][:], scalar1=1.0 / 16, scalar2=None,
                                                                              op0=ALU.mult), R=[psx_t[i]], W=[q_t[tg]])
            kb.barrier()
            with contextlib.ExitStack() as c2:
                wo = sb(c2, "x_wo", [P, 8, D], BF16)
                wo_t = Tok()
                kb.dma(POOL, wo[:], w_co_d[l].rearrange("(c p) n -> p c n", p=P), W=[wo_t])
                pss = [pst(c2, "x_pss%d" % i, [P, 512], F32) for i in range(2)]
                pss_t = toks(2)
                pstb = pst(c2, "x_pstb", [P, 1024], BF16)
                pstb_t = Tok()
                pso = [pst(c2, "x_pso%d" % i, [P, 512], F32) for i in range(2)]
                pso_t = toks(2)
                psr = [pst(c2, "x_psr%d" % i, [P, 512], F32) for i in range(2)]
                psr_t = toks(2)
                pf = sb(c2, "x_p", [P, 4, NMEM], F32)
                pn = sb(c2, "x_pn", [P, 4, NMEM], BF16)
                pT = sb(c2, "x_pT", [P, 8, P], BF16)
                oT = sb(c2, "x_oT", [P, 8, P], BF16)
                sm = sb(c2, "x_sm", [P, 12], F32)
                pf_t, pn_t, pT_t, oT_t, sm_t = Tok(), Tok(), Tok(), Tok(), Tok()
                for tt in range(NT):
                    tok = slice(tt * P, (tt + 1) * P)
                    for hh in range(4):
                        for dc in range(2):
                            kb.op(PE, lambda h, hh=hh, dc=dc: h.matmul(pss[hh // 2][:, (hh % 2) * NMEM:(hh % 2 + 1) * NMEM],
                                                                       lhsT=qT[:, 2 * hh + dc, tok], rhs=kT[:, 2 * hh + dc, :],
                                                                       start=(dc == 0), stop=(dc == 1)),
                                  R=[q_t[tt // 4], kv_t], W=[pss_t[hh // 2]])
                    for i in range(2):
                        kb.op(DVE, lambda h, i=i: h.tensor_reduce(out=sm[:, 2 * i:2 * i + 2],
                                                                  in_=pss[i][:].rearrange("p (g m) -> p g m", g=2),
                                                                  axis=AX.X, op=ALU.max), R=[pss_t[i]], W=[sm_t])
                    kb.op(DVE, lambda h: h.tensor_scalar(out=sm[:, 4:8], in0=sm[:, 0:4], scalar1=-1.0, scalar2=None, op0=ALU.mult),
                          R=[sm_t], W=[sm_t])
                    for hh in range(4):
                        kb.op(ACT, lambda h, hh=hh: h.activation(out=pf[:, hh, :], in_=pss[hh // 2][:, (hh % 2) * NMEM:(hh % 2 + 1) * NMEM],
                                                                 func=AF.Exp, bias=sm[:, 4 + hh:5 + hh], accum_out=sm[:, 8 + hh:9 + hh]),
                              R=[pss_t[hh // 2], sm_t], W=[pf_t, sm_t])
                    kb.op(DVE, lambda h: h.reciprocal(out=sm[:, 8:12], in_=sm[:, 8:12]), R=[sm_t], W=[sm_t])
                    kb.op(DVE, lambda h: h.tensor_tensor(out=pn[:], in0=pf[:], in1=sm[:, 8:12, None].to_broadcast([P, 4, NMEM]),
                                                         op=ALU.mult), R=[pf_t, sm_t], W=[pn_t])
                    for hh in range(4):
                        for mc in range(2):
                            j = hh * 2 + mc
                            kb.op(PE, lambda h, hh=hh, mc=mc, j=j: h.transpose(out=pstb[:, j * P:(j + 1) * P],
                                                                               in_=pn[:, hh, mc * P:(mc + 1) * P], identity=ident_b[:]),
                                  R=[pn_t, c_tok], W=[pstb_t])
                    kb.op(ACT, lambda h: h.copy(out=pT[:].rearrange("p j t -> p (j t)"), in_=pstb[:]), R=[pstb_t], W=[pT_t])
                    for cc in range(8):
                        hh = cc // 2
                        for mc in range(2):
                            kb.op(PE, lambda h, cc=cc, hh=hh, mc=mc: h.matmul(pso[cc // 4][:, (cc % 4) * P:(cc % 4 + 1) * P],
                                                                              lhsT=vx[:, mc, cc * P:(cc + 1) * P], rhs=pT[:, hh * 2 + mc, :],
                                                                              start=(mc == 0), stop=(mc == 1)),
                                  R=[kv_t, pT_t], W=[pso_t[cc // 4]])
                    for i in range(2):
                        if i == 0:
                            kb.op(ACT, lambda h, i=i: h.copy(out=oT[:, 4 * i:4 * i + 4, :].rearrange("p c t -> p (c t)"), in_=pso[i][:]),
                                  R=[pso_t[i]], W=[oT_t])
                        else:
                            kb.op(DVE, lambda h, i=i: h.tensor_copy(out=oT[:, 4 * i:4 * i + 4, :].rearrange("p c t -> p (c t)"), in_=pso[i][:]),
                                  R=[pso_t[i]], W=[oT_t])
                    for ng in range(2):
                        for k in range(8):
                            kb.op(PE, lambda h, k=k, ng=ng: h.matmul(psr[ng][:], lhsT=oT[:, k, :], rhs=wo[:, k, ng * 512:(ng + 1) * 512],
                                                                     start=(k == 0), stop=(k == 7)), R=[oT_t, wo_t], W=[psr_t[ng]])
                        dst = resid[:, tt, ng * 512:(ng + 1) * 512]
                        kb.op(DVE, lambda h, dst=dst, ng=ng: h.tensor_tensor(out=dst, in0=psr[ng][:], in1=dst, op=ALU.add),
                              R=[psr_t[ng], r_tok[tt]], W=[r_tok[tt]])
            kb.barrier()

    def peer(l):
        load_gain(norm_ffn_d[l, :])
        with contextlib.ExitStack() as ps_:
            wpq = sb(ps_, "p_wq", [P, 8, D], BF16)
            wpq_t = Tok()
            kb.dma(POOL, wpq[:], peer_wq_d[l].rearrange("(c p) n -> p c n", p=P), W=[wpq_t])
            skT = sb(ps_, "p_skT", [P, 8, P], BF16)
            skn_stack = contextlib.ExitStack()
            skn = sb(skn_stack, "p_skn", [P, 8, P], F32)
            sk_t = Tok()
            with nc.allow_non_contiguous_dma(reason="sub keys"):
                for hh in range(8):
                    for pp in range(2):
                        kb.dma(SP, skn[:, hh, pp * 64:(pp + 1) * 64], peer_sk_d[l, hh, pp], W=[sk_t])
            ps4 = [pst(ps_, "p_ps%d" % i, [P, 512], F32) for i in range(4)]
            ps4_t = toks(4)
            psS = [pst(ps_, "p_psS%d" % i, [P, 512], F32) for i in range(2)]
            psS_t = toks(2)
            psA = pst(ps_, "p_psA", [P, 1024], BF16)
            psA_t = Tok()
            for hh in range(8):
                kb.op(PE, lambda h, hh=hh: h.transpose(out=ps4[hh // 4][:, (hh % 4) * P:(hh % 4 + 1) * P], in_=skn[:, hh, :],
                                                       identity=ident_f[:]), R=[sk_t, c_tok], W=[ps4_t[hh // 4]])
            for i in range(2):
                kb.op(ACT, lambda h, i=i: h.copy(out=skT[:, 4 * i:4 * i + 4, :].rearrange("p g n -> p (g n)"), in_=ps4[i][:]),
                      R=[ps4_t[i]], W=[sk_t])
            kb.barrier()
            skn_stack.close()
            xn = sb(ps_, "p_xn", [P, D], BF16)
            junk = sb(ps_, "p_junk", [P, D], BF16)
            hTt = sb(ps_, "p_hT", [P, 8, P], BF16)
            qpT = sb(ps_, "p_qpT", [P, 8, P], BF16)
            ssb = sb(ps_, "p_s", [P, 16, P], F32)
            top = sb(ps_, "p_top", [P, 16, 16], F32)
            c16 = sb(ps_, "p_c16", [P, 8, 16], F32)
            sm = sb(ps_, "p_sm", [P, 32], F32)
            IG = 16
            NIG = P // IG
            zb = [sb(ps_, "p_zb%d" % i, [P, IG, P], F32) for i in range(2)]
            eb = [sb(ps_, "p_eb%d" % i, [P, IG * P], BF16) for i in range(2)]
            gh = [sb(ps_, "p_gh%d" % i, [P, IG * P], BF16) for i in range(2)]
            zb_t, eb_t, gh_t = toks(2), toks(2), toks(2)
            cand = zb[0][:].rearrange("p i j -> p (i j)").rearrange("p (h c) -> p h c", h=8)
            G = sb(ps_, "p_G", [P, NEXP], BF16)
            G_t = toks(NIG)
            ut = [sb(ps_, "p_ut%d" % i, [P, 8, 512], BF16) for i in range(2)]
            vt = [sb(ps_, "p_vt%d" % i, [P, 4, D], BF16) for i in range(2)]
            ge = [sb(ps_, "p_ge%d" % i, [P, 512], BF16) for i in range(2)]
            Ab = [sb(ps_, "p_A%d" % i, [P, 512], BF16) for i in range(2)]
            AT = [sb(ps_, "p_AT%d" % i, [P, 4, P], BF16) for i in range(2)]
            ge_t, Ab_t, AT_t = toks(2), toks(2), toks(2)
            psA2 = pst(ps_, "p_psA2", [P, 1024], BF16)
            psAs = [psA, psA2]
            psAs_t = [psA_t, Tok()]
            names = ["xn", "junk", "hTt", "qpT", "ssb", "sm"]
            T = {k_: Tok() for k_ in names}
            top_t, c16_t, R_t, ej_t = toks(16), toks(8), toks(16), toks(8)
            ej = sb(ps_, "p_ej", [P, 8, 16], F32)
            T["cand"] = zb_t[0]
            ut_t, vt_t = toks(2), toks(2)
            for tt in range(PEER_TILES):
                norm_tile(resid[:, tt, :], r_tok[tt], tt, xn, T["xn"], psA, psA_t, hTt[:], T["hTt"], junk, T["junk"])
                for cc in range(8):
                    i = cc // 4
                    for k in range(8):
                        kb.op(PE, lambda h, cc=cc, k=k, i=i: h.matmul(ps4[i][:, (cc % 4) * P:(cc % 4 + 1) * P], lhsT=wpq[:, k, cc * P:(cc + 1) * P],
                                                                      rhs=hTt[:, k, :], start=(k == 0), stop=(k == 7)),
                              R=[wpq_t, T["hTt"]], W=[ps4_t[i]])
                for i in range(2):
                    kb.op(ACT, lambda h, i=i: h.copy(out=qpT[:, 4 * i:4 * i + 4, :].rearrange("p c t -> p (c t)"), in_=ps4[i][:]),
                          R=[ps4_t[i]], W=[T["qpT"]])
                for hh in range(8):
                    for pp in range(2):
                        g = pp * 8 + hh
                        kb.op(PE, lambda h, hh=hh, pp=pp, g=g: h.matmul(ps4[g // 4][:, (g % 4) * P:(g % 4 + 1) * P],
                                                                        lhsT=qpT[pp * 64:(pp + 1) * 64, hh, :],
                                                                        rhs=skT[pp * 64:(pp + 1) * 64, hh, :], start=True, stop=True),
                              R=[T["qpT"], sk_t], W=[ps4_t[g // 4]])
                for i in range(4):
                    kb.op(ACT, lambda h, i=i: h.copy(out=ssb[:, 4 * i:4 * i + 4, :].rearrange("p g n -> p (g n)"), in_=ps4[i][:]),
                          R=[ps4_t[i]], W=[T["ssb"]])
                if PEER_UPTO < 2:
                    continue
                scrA = zb[1][:]
                scrB = zb[1][:].rearrange("p i j -> p (i j)").rearrange("p (h c) -> p h c", h=8)
                for g in range(16):
                    kb.op(DVE, lambda h, g=g: h.max(out=top[:, g, 0:8], in_=ssb[:, g, :]), R=[T["ssb"]],
                          W=[top_t[g]] + ([zb_t[1]] if g == 0 else []))
                for g in range(16):
                    kb.op(DVE, lambda h, g=g: h.match_replace(out=scrA[:, g, :], in_to_replace=top[:, g, 0:8], in_values=ssb[:, g, :],
                                                              imm_value=NEG), R=[T["ssb"], top_t[g]], W=[R_t[g]])
                for g in range(16):
                    kb.op(DVE, lambda h, g=g: h.max(out=top[:, g, 8:16], in_=scrA[:, g, :]), R=[R_t[g]], W=[top_t[g]])
                kb.op(DVE, lambda h: h.tensor_tensor(out=cand.rearrange("p h (a b) -> p h a b", a=16),
                                                     in0=top[:, 0:8, :, None].to_broadcast([P, 8, 16, 16]),
                                                     in1=top[:, 8:16, None, :].to_broadcast([P, 8, 16, 16]), op=ALU.add),
                      R=top_t, W=[T["cand"]])
                for hh in range(8):
                    kb.op(DVE, lambda h, hh=hh: h.max(out=c16[:, hh, 0:8], in_=cand[:, hh, :]), R=[T["cand"]], W=[c16_t[hh]])
                for hh in range(8):
                    kb.op(DVE, lambda h, hh=hh: h.match_replace(out=scrB[:, hh, :], in_to_replace=c16[:, hh, 0:8], in_values=cand[:, hh, :],
                                                                imm_value=NEG), R=[T["cand"], c16_t[hh]], W=[R_t[2 * hh], R_t[2 * hh + 1]])
                for hh in range(8):
                    kb.op(DVE, lambda h, hh=hh: h.max(out=c16[:, hh, 8:16], in_=scrB[:, hh, :]), R=[R_t[2 * hh], R_t[2 * hh + 1]], W=[c16_t[hh]])
                kb.op(DVE, lambda h: h.tensor_copy(out=sm[:, 0:8], in_=c16[:, :, 15]), R=c16_t, W=[T["sm"]])
                kb.op(DVE, lambda h: h.tensor_scalar(out=sm[:, 8:16], in0=c16[:, :, 0], scalar1=-1.0, scalar2=None, op0=ALU.mult),
                      R=c16_t, W=[T["sm"]])
                for hh in range(8):
                    kb.op(ACT, lambda h, hh=hh: h.activation(out=ej[:, hh, :], in_=c16[:, hh, :], func=AF.Exp, bias=sm[:, 8 + hh:9 + hh],
                                                             accum_out=sm[:, 16 + hh:17 + hh]), R=[c16_t[hh], T["sm"]], W=[ej_t[hh], T["sm"]])
                kb.op(ACT, lambda h: h.activation(out=sm[:, 16:24], in_=sm[:, 16:24], func=AF.Ln), R=[T["sm"]], W=[T["sm"]])
                kb.op(DVE, lambda h: h.tensor_tensor(out=sm[:, 24:32], in0=sm[:, 8:16], in1=sm[:, 16:24], op=ALU.subtract),
                      R=[T["sm"]] + R_t, W=[T["sm"], zb_t[1]])
                if PEER_UPTO < 3:
                    continue
                NEG_ = NEXP // 512
                cnt = [0]

                units = [(ig_, hh_) for ig_ in range(NIG) for hh_ in range(8)]

                def unit_A(n):
                    ig, hh = units[n]
                    u = n % 2
                    kb.op(DVE, lambda h, hh=hh, ig=ig, u=u: h.tensor_tensor(
                        out=zb[u][:], in0=ssb[:, hh, ig * IG:(ig + 1) * IG, None].to_broadcast([P, IG, P]),
                        in1=ssb[:, 8 + hh, None, :].to_broadcast([P, IG, P]), op=ALU.add),
                        R=[T["ssb"]], W=[zb_t[u]])
                    kb.op(ACT, lambda h, hh=hh, u=u: h.activation(out=eb[u][:], in_=zb[u][:].rearrange("p i j -> p (i j)"), func=AF.Exp,
                                                                  bias=sm[:, 24 + hh:25 + hh]), R=[zb_t[u], T["sm"]], W=[eb_t[u]])

                def unit_C(n):
                    ig, hh = units[n]
                    u = n % 2
                    Gblk = G[:, ig * IG * P:(ig + 1) * IG * P]
                    if hh == 0:
                        kb.op(DVE, lambda h, hh=hh, Gblk=Gblk, u=u: h.scalar_tensor_tensor(
                            out=Gblk, in0=zb[u][:].rearrange("p i j -> p (i j)"), scalar=sm[:, hh:hh + 1], in1=eb[u][:],
                            op0=ALU.is_ge, op1=ALU.mult), R=[zb_t[u], eb_t[u], T["sm"]], W=[G_t[ig]])
                    else:
                        kb.op(DVE, lambda h, hh=hh, u=u: h.scalar_tensor_tensor(
                            out=gh[u][:], in0=zb[u][:].rearrange("p i j -> p (i j)"), scalar=sm[:, hh:hh + 1], in1=eb[u][:],
                            op0=ALU.is_ge, op1=ALU.mult), R=[zb_t[u], eb_t[u], T["sm"]], W=[gh_t[u]])
                        kb.op(DVE, lambda h, Gblk=Gblk, u=u: h.tensor_tensor(out=Gblk, in0=Gblk, in1=gh[u][:], op=ALU.add),
                              R=[gh_t[u], G_t[ig]], W=[G_t[ig]])

                nun = [0]

                def emit_units(k):
                    for _ in range(k):
                        n = nun[0]
                        if n >= len(units):
                            return
                        if n == 0:
                            unit_A(0)
                        if n + 1 < len(units):
                            unit_A(n + 1)
                        unit_C(n)
                        nun[0] += 1

                def stage_S(eg):
                    i = eg % 2
                    kb.dma(SP, ut[i][:], ut_d[l, :, eg * 512:(eg + 1) * 512].rearrange("(c p) e -> p c e", p=P), W=[ut_t[i]])
                    kb.dma(SP, vt[i][:], vb_d[l, eg * 512:(eg + 1) * 512, :].rearrange("(c p) n -> p c n", p=P), W=[vt_t[i]])
                    for k in range(8):
                        kb.op(PE, lambda h, k=k, i=i: h.matmul(psS[i][:], lhsT=hTt[:, k, :], rhs=ut[i][:, k, :], start=(k == 0), stop=(k == 7)),
                              R=[T["hTt"], ut_t[i]], W=[psS_t[i]])
                    kb.op(ACT, lambda h, i=i: h.activation(out=ge[i][:], in_=psS[i][:], func=AF.Gelu_apprx_tanh), R=[psS_t[i]], W=[ge_t[i]])
                    kb.op(DVE, lambda h, eg=eg, i=i: h.tensor_tensor(out=Ab[i][:], in0=ge[i][:], in1=G[:, eg * 512:(eg + 1) * 512], op=ALU.mult),
                          R=[ge_t[i], G_t[(eg * 512) // (IG * P)]], W=[Ab_t[i]])

                def stage_T(eg):
                    i = eg % 2
                    for c in range(4):
                        kb.op(PE, lambda h, c=c, i=i: h.transpose(out=psAs[i][:, c * P:(c + 1) * P], in_=Ab[i][:, c * P:(c + 1) * P], identity=ident_b[:]),
                              R=[Ab_t[i], c_tok], W=[psAs_t[i]])
                    kb.op(ACT, lambda h, i=i: h.copy(out=AT[i][:].rearrange("p c t -> p (c t)"), in_=psAs[i][:, 0:512]), R=[psAs_t[i]], W=[AT_t[i]])

                def stage_V(eg):
                    i = eg % 2
                    for c in range(4):
                        for ng in range(2):
                            kb.op(PE, lambda h, c=c, ng=ng, i=i, eg=eg: h.matmul(ps4[2 + ng][:], lhsT=AT[i][:, c, :], rhs=vt[i][:, c, ng * 512:(ng + 1) * 512],
                                                                                 start=(eg == 0 and c == 0), stop=(eg == NEG_ - 1 and c == 3)),
                                  R=[AT_t[i], vt_t[i]], W=[ps4_t[2 + ng]])

                emit_units(8)
                for step in range(NEG_ + 2):
                    if step % 2 == 0:
                        emit_units(4)
                    if step >= 2:
                        stage_V(step - 2)
                    if 1 <= step <= NEG_:
                        stage_T(step - 1)
                    if step < NEG_:
                        stage_S(step)
                for ng in range(2):
                    dst = resid[:, tt, ng * 512:(ng + 1) * 512]
                    kb.op(DVE, lambda h, dst=dst, ng=ng: h.tensor_tensor(out=dst, in0=ps4[2 + ng][:], in1=dst, op=ALU.add),
                          R=[ps4_t[2 + ng], r_tok[tt]], W=[r_tok[tt]])
        kb.barrier()

    stage = (dbg or {}).get("stage", "full")
    need_peer = stage in ("full", "peer0", "prep", "peeronly")
    if need_peer and not (dbg or {}).get("noprep"):
        prep_peer_weights()
    for b in range(nseq):
        for q in range(4):
            kb.dma(SP, resid[:, q * 4:(q + 1) * 4, :], x_d[b, q * 512:(q + 1) * 512, :].rearrange("(t p) d -> p t d", p=P),
                   W=r_tok[q * 4:(q + 1) * 4])
        done = stage == "prep"
        if stage == "peeronly":
            peer(0)
            done = True
        for l in range(depth if not done else 0):
            with contextlib.ExitStack() as ls:
                hT = sb(ls, "hT", [P, 8, S], BF16)
                h_tok = toks(NT)
                norm_seq(norm_mix_d[l, :], hT, h_tok)
                mixer(l, hT, h_tok)
                if stage == "mix0":
                    done = True
                if not done:
                    norm_seq(norm_mem_d[l, :], hT, h_tok)
                    cross_attn(l, b, hT, h_tok)
                    if stage == "xattn0":
                        done = True
            if done:
                break
            peer(l)
            if stage == "peer0":
                done = True
                break
        if not done:
            with contextlib.ExitStack() as fs:
                fg = sb(fs, "fg", [P, D], F32)
                fg_t = Tok()
                junk = sb(fs, "f_junk", [P, D], BF16)
                junk_t = Tok()
                with nc.allow_non_contiguous_dma(reason="gain bcast"):
                    kb.dma(SP, fg[:], final_gain_d.partition_broadcast(P), W=[fg_t])
                for tt in range(NT):
                    col = rstd[:, tt:tt + 1]
                    kb.op(ACT, lambda h, tt=tt: h.activation(out=junk[:], in_=resid[:, tt, :], func=AF.Square, accum_out=ssq[:, tt:tt + 1]),
                          R=[r_tok[tt]], W=[junk_t, n_tok])
                    kb.op(DVE, lambda h, tt=tt, col=col: h.tensor_scalar(out=col, in0=ssq[:, tt:tt + 1], scalar1=1.0 / D, scalar2=EPS,
                                                                        op0=ALU.mult, op1=ALU.add), R=[n_tok], W=[n_tok])
                    kb.op(ACT, lambda h, col=col: h.activation(out=col, in_=col, func=AF.Sqrt), R=[n_tok], W=[n_tok])
                    kb.op(DVE, lambda h, col=col: h.reciprocal(out=col, in_=col), R=[n_tok], W=[n_tok])
                    kb.op(DVE, lambda h, tt=tt, col=col: h.scalar_tensor_tensor(out=resid[:, tt, :], in0=resid[:, tt, :], scalar=col, in1=fg[:],
                                                                               op0=ALU.mult, op1=ALU.mult),
                          R=[r_tok[tt], n_tok, fg_t], W=[r_tok[tt]])
        for q in range(4):
            kb.dma(SP, out_d[b, q * 512:(q + 1) * 512, :].rearrange("(t p) d -> p t d", p=P), resid[:, q * 4:(q + 1) * 4, :],
                   R=r_tok[q * 4:(q + 1) * 4])
        kb.barrier()
    kb.barrier()
    return nc, kb


def _consts():
    p = np.arange(P)[:, None]
    f = np.arange(P)[None, :]
    return {
        "c_ident": (p == f).astype(np.float32),
        "c_mlt": (f < p).astype(np.float32),
        "c_mle": (f <= p).astype(np.float32),
        "c_mge": (f >= p).astype(np.float32),
        "c_hm": (np.arange(P)[:, None] // 32 == np.arange(4)[None, :]).astype(np.float32),
    }


_CACHE = {}


def kernel(**inputs):
    ncores = 8
    nseq = 2
    if "nc" not in _CACHE:
        _CACHE["nc"] = build(nseq=nseq)[0]
    nc = _CACHE["nc"]
    cst = _consts()
    in_maps = []
    for c in range(ncores):
        m = {}
        for k, v in inputs.items():
            v = np.ascontiguousarray(np.asarray(v, dtype=np.float32))
            if k in ("x", "mem"):
                m[k] = np.ascontiguousarray(v[c * nseq:(c + 1) * nseq])
            else:
                m[k] = v
        m.update(cst)
        in_maps.append(m)
    res = run_bass_kernel_spmd(nc, in_maps, core_ids=list(range(ncores)))
    return np.concatenate([np.asarray(r["out"], dtype=np.float32) for r in res.results], axis=0)
```

```python
import contextlib
import numpy as np
import concourse.bass as bass
import concourse.mybir as mybir
from concourse.bass_utils import run_bass_kernel_spmd

F32 = mybir.dt.float32
BF16 = mybir.dt.bfloat16
AF = mybir.ActivationFunctionType
ALU = mybir.AluOpType
AX = mybir.AxisListType

P = 128
S = 2048
D = 1024
NT = S // P
DEPTH = 2
NMEM = 256
NEXP = 16384
EPS = 1e-6
NEG = -1.0e30


class Tok:
    __slots__ = ("w", "r")

    def __init__(self):
        self.w = None
        self.r = {}


class Eng:
    def __init__(self, kb, h, name):
        self.h = h
        self.name = name
        self.sid = kb.new_sem(name)
        self.cnt = 0
        self.seen = {}


class KB:
    def __init__(self, nc):
        self.nc = nc
        self.es = contextlib.ExitStack()
        self.sems = []
        self.PE = Eng(self, nc.tensor, "pe")
        self.ACT = Eng(self, nc.scalar, "act")
        self.DVE = Eng(self, nc.vector, "dve")
        self.POOL = Eng(self, nc.gpsimd, "pool")
        self.SP = Eng(self, nc.sync, "sp")
        self.engs = [self.PE, self.ACT, self.DVE, self.POOL, self.SP]
        self.dsem = [self.new_sem("dma%d" % i) for i in range(28)]
        self.dtgt = {s: 0 for s in self.dsem}
        self.dnext = 0
        self.n_ins = 0

    def new_sem(self, name):
        s = self.es.enter_context(self.nc.semaphore(name))
        self.sems.append(s)
        return len(self.sems) - 1

    def wait(self, eng, sid, val):
        if val <= 0 or eng.seen.get(sid, 0) >= val:
            return
        eng.h.wait_ge(self.sems[sid], val)
        eng.seen[sid] = val

    def _deps(self, eng, R, W):
        deps = {}
        for t in R:
            if t.w is not None:
                s, v = t.w
                if deps.get(s, 0) < v:
                    deps[s] = v
        for t in W:
            if t.w is not None:
                s, v = t.w
                if deps.get(s, 0) < v:
                    deps[s] = v
            for s, v in t.r.items():
                if deps.get(s, 0) < v:
                    deps[s] = v
        for s, v in deps.items():
            if eng is self.PE and s == self.PE.sid:
                continue
            self.wait(eng, s, v)

    def _mark(self, me, R, W):
        s, v = me
        for t in W:
            t.w = me
            t.r = {}
        for t in R:
            if t.r.get(s, 0) < v:
                t.r[s] = v

    def op(self, eng, fn, R=(), W=()):
        self._deps(eng, R, W)
        ins = fn(eng.h)
        eng.cnt += 1
        ins.then_inc(self.sems[eng.sid], 1)
        self._mark((eng.sid, eng.cnt), R, W)
        self.n_ins += 1

    def dma(self, eng, out, in_, R=(), W=(), **kw):
        self._deps(eng, R, W)
        sid = self.dsem[self.dnext % len(self.dsem)]
        self.dnext += 1
        self.wait(eng, sid, self.dtgt[sid])
        self.dtgt[sid] += 16
        eng.h.dma_start(out=out, in_=in_, **kw).then_inc(self.sems[sid], 16)
        self._mark((sid, self.dtgt[sid]), R, W)
        self.n_ins += 1

    def barrier(self):
        for e in self.engs:
            for f in self.engs:
                if f is not e:
                    self.wait(e, f.sid, f.cnt)
            for sid in self.dsem:
                self.wait(e, sid, self.dtgt[sid])


def toks(n):
    return [Tok() for _ in range(n)]


def build(nseq=2, depth=DEPTH, dbg=None):
    nc = bass.Bass("TRN2", target_bir_lowering=False)
    kb = KB(nc)
    es = kb.es
    PE, ACT, DVE, POOL, SP = kb.PE, kb.ACT, kb.DVE, kb.POOL, kb.SP

    def din(name, shape):
        return nc.dram_tensor(name, list(shape), F32, kind="ExternalInput").ap()

    x_d = din("x", [nseq, S, D])
    mem_d = din("mem", [nseq, NMEM, D])
    norm_mix_d = din("norm_mix", [DEPTH, D])
    w_in_d = din("w_in", [DEPTH, D, 2832])
    sg_v_gain_d = din("sg_v_gain", [DEPTH, 256])
    sg_w_d = din("sg_w_spatial", [DEPTH, 4, P, P])
    sg_b_d = din("sg_b_spatial", [DEPTH, 4, P])
    gla_wg_d = din("gla_w_gate", [DEPTH, 16, P])
    gla_bg_d = din("gla_b_gate", [DEPTH, P])
    gla_og_d = din("gla_out_gain", [DEPTH, 256])
    w_out_d = din("w_out", [DEPTH, D, D])
    norm_mem_d = din("norm_mem", [DEPTH, D])
    mem_gain_d = din("mem_gain", [DEPTH, D])
    w_cq_d = din("w_cq", [DEPTH, D, D])
    w_ckv_d = din("w_ckv", [DEPTH, D, 2 * D])
    w_co_d = din("w_co", [DEPTH, D, D])
    norm_ffn_d = din("norm_ffn", [DEPTH, D])
    peer_wq_d = din("peer_w_q", [DEPTH, D, D])
    peer_sk_d = din("peer_sub_keys", [DEPTH, 8, 2, P, 64])
    peer_u_d = din("peer_u", [DEPTH, NEXP, D])
    peer_v_d = din("peer_v", [DEPTH, NEXP, D])
    final_gain_d = din("final_gain", [D])
    ident_d = din("c_ident", [P, P])
    mlt_d = din("c_mlt", [P, P])
    mle_d = din("c_mle", [P, P])
    mge_d = din("c_mge", [P, P])
    hm_d = din("c_hm", [P, 4])
    out_d = nc.dram_tensor("out", [nseq, S, D], F32, kind="ExternalOutput").ap()
    scratch_kind = "ExternalOutput" if (dbg or {}).get("dump_prep") else "Internal"
    ut_d = nc.dram_tensor("ut_bf", [DEPTH, D, NEXP], BF16, kind=scratch_kind).ap()
    vb_d = nc.dram_tensor("v_bf", [DEPTH, NEXP, D], BF16, kind=scratch_kind).ap()
    PEER_TILES = (dbg or {}).get("peer_tiles", NT)
    PEER_UPTO = (dbg or {}).get("peer_upto", 9)

    uid = [0]

    def sb(stack, name, shape, dt):
        uid[0] += 1
        return stack.enter_context(nc.sbuf_tensor("%s_%d" % (name, uid[0]), list(shape), dt))

    def pst(stack, name, shape, dt):
        uid[0] += 1
        return stack.enter_context(nc.psum_tensor("%s_%d" % (name, uid[0]), list(shape), dt))

    resid = sb(es, "resid", [P, NT, D], F32)
    r_tok = toks(NT)
    ident_f = sb(es, "ident_f", [P, P], F32)
    ident_b = sb(es, "ident_b", [P, P], BF16)
    mlt_f = sb(es, "mlt_f", [P, P], F32)
    mlt_b = sb(es, "mlt_b", [P, P], BF16)
    mle_f = sb(es, "mle_f", [P, P], F32)
    mge_f = sb(es, "mge_f", [P, P], F32)
    ones_f = sb(es, "ones_f", [P, P], F32)
    hm = sb(es, "hm", [P, 4], F32)
    gains = sb(es, "gains", [P, 8], F32)
    rstd = sb(es, "rstd", [P, NT], F32)
    ssq = sb(es, "ssq", [P, NT], F32)
    c_tok = Tok()
    g_tok = Tok()
    n_tok = Tok()

    kb.dma(SP, ident_f[:], ident_d[:, :], W=[c_tok])
    kb.dma(SP, mlt_f[:], mlt_d[:, :], W=[c_tok])
    kb.dma(SP, mle_f[:], mle_d[:, :], W=[c_tok])
    kb.dma(SP, mge_f[:], mge_d[:, :], W=[c_tok])
    kb.dma(SP, hm[:], hm_d[:, :], W=[c_tok])
    kb.dma(POOL, ident_b[:], ident_d[:, :], W=[c_tok])
    kb.dma(POOL, mlt_b[:], mlt_d[:, :], W=[c_tok])
    kb.op(DVE, lambda h: h.memset(ones_f[:], 1.0), W=[c_tok])

    with contextlib.ExitStack() as pes:
        pass

    def load_gain(vec_ap):
        with nc.allow_non_contiguous_dma(reason="tiny gain load"):
            kb.dma(SP, gains[:], vec_ap.rearrange("(c p) -> p c", p=P), W=[g_tok])

    def norm_tile(src_ap, src_tok, tt_idx, xn, xn_tok, pt_b, pt_tok, dst_ap, dst_tok, junk, junk_tok):
        col = rstd[:, tt_idx:tt_idx + 1]
        kb.op(ACT, lambda h: h.activation(out=junk[:], in_=src_ap, func=AF.Square, accum_out=ssq[:, tt_idx:tt_idx + 1]),
              R=[src_tok], W=[junk_tok, n_tok])
        kb.op(DVE, lambda h: h.tensor_scalar(out=col, in0=ssq[:, tt_idx:tt_idx + 1], scalar1=1.0 / D, scalar2=EPS,
                                             op0=ALU.mult, op1=ALU.add), R=[n_tok], W=[n_tok])
        kb.op(ACT, lambda h: h.activation(out=col, in_=col, func=AF.Sqrt), R=[n_tok], W=[n_tok])
        kb.op(DVE, lambda h: h.reciprocal(out=col, in_=col), R=[n_tok], W=[n_tok])
        kb.op(DVE, lambda h: h.tensor_scalar(out=xn[:], in0=src_ap, scalar1=col, scalar2=None, op0=ALU.mult),
              R=[src_tok, n_tok], W=[xn_tok])
        for c in range(8):
            kb.op(PE, lambda h, c=c: h.transpose(out=pt_b[:, c * P:(c + 1) * P], in_=xn[:, c * P:(c + 1) * P],
                                                 identity=ident_b[:]), R=[xn_tok, c_tok], W=[pt_tok])
        kb.op(DVE, lambda h: h.tensor_tensor(out=dst_ap, in0=pt_b[:].rearrange("p (c t) -> p c t", c=8),
                                             in1=gains[:, :, None].to_broadcast([P, 8, P]), op=ALU.mult),
              R=[pt_tok, g_tok], W=[dst_tok])

    def prep_peer_weights():
        with contextlib.ExitStack() as ps_:
            ub = [sb(ps_, "pp_ub%d" % i, [P, D], BF16) for i in range(2)]
            ub_t = toks(2)
            stg = [sb(ps_, "pp_stg%d" % i, [P, 8, 512], BF16) for i in range(2)]
            stg_t = toks(2)
            ptb = [pst(ps_, "pp_pt%d" % i, [P, 1024], BF16) for i in range(2)]
            ptb_t = toks(2)
            vst = [sb(ps_, "pp_vst%d" % i, [P, 4, D], BF16) for i in range(2)]
            vst_t = toks(2)
            for l in range(depth):
                for r in range((dbg or {}).get('prep_nv', NEXP // 512)):
                    i = r % 2
                    kb.dma(POOL, vst[i][:], peer_v_d[l, r * 512:(r + 1) * 512, :].rearrange("(c p) n -> p c n", p=P), W=[vst_t[i]])
                    kb.dma(SP, vb_d[l, r * 512:(r + 1) * 512, :].rearrange("(c p) n -> p c n", p=P), vst[i][:], R=[vst_t[i]])
                for et in range((dbg or {}).get('prep_nu', NEXP // P)):
                    i = et % 2
                    g = (et // 4) % 2
                    kb.dma(POOL, ub[i][:], peer_u_d[l, et * P:(et + 1) * P, :], W=[ub_t[i]])
                    for c in range(8):
                        kb.op(PE, lambda h, c=c, i=i: h.transpose(out=ptb[i][:, c * P:(c + 1) * P],
                                                                 in_=ub[i][:, c * P:(c + 1) * P], identity=ident_b[:]),
                              R=[ub_t[i], c_tok], W=[ptb_t[i]])
                    e2 = ACT if et % 2 == 0 else DVE
                    dst = stg[g][:, :, (et % 4) * P:(et % 4 + 1) * P]
                    src = ptb[i][:].rearrange("p (c e) -> p c e", c=8)
                    if e2 is ACT:
                        kb.op(ACT, lambda h, dst=dst, src=src: h.copy(out=dst, in_=src), R=[ptb_t[i]], W=[stg_t[g]])
                    else:
                        kb.op(DVE, lambda h, dst=dst, src=src: h.tensor_copy(out=dst, in_=src), R=[ptb_t[i]], W=[stg_t[g]])
                    if et % 4 == 3:
                        eg = et // 4
                        kb.dma(SP, ut_d[l, :, eg * 512:(eg + 1) * 512].rearrange("(c p) e -> p c e", p=P), stg[g][:],
                               R=[stg_t[g]])
        kb.barrier()

    def mixer(l, hT, h_tok):
        with contextlib.ExitStack() as ms:
            qkT = sb(ms, "qkT", [P, 8, S], BF16)
            qk_tok = toks(8)
            v_sb = sb(ms, "v_sb", [P, NT, 512], BF16)
            v_tok = toks(NT)
            with contextlib.ExitStack() as m1:
                wA = sb(m1, "wA", [P, 8, 1024], BF16)
                wA_tok = Tok()
                kb.dma(POOL, wA[:], w_in_d[l, :, 0:1024].rearrange("(c p) n -> p c n", p=P), W=[wA_tok])
                psq = [pst(m1, "psq%d" % i, [P, 512], F32) for i in range(2)]
                psq_t = toks(2)
                n = 0
                for cc in range(8):
                    for tg in range(4):
                        i = n % 2
                        n += 1
                        for k in range(8):
                            kb.op(PE, lambda h, k=k, i=i, cc=cc, tg=tg: h.matmul(
                                psq[i][:], lhsT=wA[:, k, cc * P:(cc + 1) * P], rhs=hT[:, k, tg * 512:(tg + 1) * 512],
                                start=(k == 0), stop=(k == 7)),
                                R=[wA_tok] + h_tok[tg * 4:tg * 4 + 4], W=[psq_t[i]])
                        sc = 0.125 if cc < 4 else 1.0
                        dst = qkT[:, cc, tg * 512:(tg + 1) * 512]
                        if n % 2 == 0:
                            kb.op(ACT, lambda h, dst=dst, i=i, sc=sc: h.mul(out=dst, in_=psq[i][:], mul=sc),
                                  R=[psq_t[i]], W=[qk_tok[cc]])
                        else:
                            kb.op(DVE, lambda h, dst=dst, i=i, sc=sc: h.tensor_scalar(out=dst, in0=psq[i][:], scalar1=sc,
                                                                                    scalar2=None, op0=ALU.mult),
                                  R=[psq_t[i]], W=[qk_tok[cc]])
            kb.barrier()
            with contextlib.ExitStack() as m2:
                wB = sb(m2, "wB", [P, 8, 1536], BF16)
                wC = sb(m2, "wC", [P, 8, 272], BF16)
                w_tok = Tok()
                kb.dma(POOL, wB[:, :, 0:1024], w_in_d[l, :, 1024:2048].rearrange("(c p) n -> p c n", p=P), W=[w_tok])
                kb.dma(POOL, wB[:, :, 1024:1536], w_in_d[l, :, 2304:2816].rearrange("(c p) n -> p c n", p=P), W=[w_tok])
                kb.dma(POOL, wC[:, :, 0:256], w_in_d[l, :, 2048:2304].rearrange("(c p) n -> p c n", p=P), W=[w_tok])
                with nc.allow_non_contiguous_dma(reason="small"):
                    kb.dma(POOL, wC[:, :, 256:272], w_in_d[l, :, 2816:2832].rearrange("(c p) n -> p c n", p=P), W=[w_tok])
                sgg = sb(m2, "sgg", [P, 256], F32)
                glg = sb(m2, "glg", [P, 256], F32)
                bsT = sb(m2, "bsT", [P, 4], F32)
                nbg = sb(m2, "nbg", [P, 1], F32)
                wgt = sb(m2, "wgt", [16, P], BF16)
                wsn = sb(m2, "wsn", [P, 4, P], F32)
                wsT = sb(m2, "wsT", [P, 4, P], BF16)
                sp_tok = Tok()
                with nc.allow_non_contiguous_dma(reason="small params"):
                    kb.dma(SP, sgg[:], sg_v_gain_d[l, :].partition_broadcast(P), W=[sp_tok])
                    kb.dma(SP, glg[:], gla_og_d[l, :].partition_broadcast(P), W=[sp_tok])
                    kb.dma(SP, bsT[:], sg_b_d[l].rearrange("g t -> t g"), W=[sp_tok])
                    kb.dma(SP, nbg[:], gla_bg_d[l, :].rearrange("(p o) -> p o", o=1), W=[sp_tok])
                    kb.dma(POOL, wgt[:], gla_wg_d[l], W=[sp_tok])
                    kb.dma(SP, wsn[:], sg_w_d[l].rearrange("g t s -> t g s"), W=[sp_tok])
                kb.op(DVE, lambda h: h.tensor_scalar(out=nbg[:], in0=nbg[:], scalar1=-1.0, scalar2=None, op0=ALU.mult),
                      R=[sp_tok], W=[sp_tok])
                kb.op(DVE, lambda h: h.tensor_tensor(out=wsn[:], in0=wsn[:], in1=mle_f[:, None, :].to_broadcast([P, 4, P]),
                                                     op=ALU.mult), R=[sp_tok, c_tok], W=[sp_tok])
                ps_tm = [pst(m2, "ps_tm%d" % i, [P, 512], F32) for i in range(3)]
                ps_tm_t = toks(3)
                ps_f = pst(m2, "ps_f", [P, 512], F32)
                ps_f_t = Tok()
                ps_g = pst(m2, "ps_g", [P, 512], F32)
                ps_g_t = Tok()
                ps_s = pst(m2, "ps_s", [P, 512], F32)
                ps_s_t = Tok()
                ps_b = pst(m2, "ps_b", [P, 1024], BF16)
                ps_b_t = Tok()
                ps_st = pst(m2, "ps_st", [P, 512], F32)
                ps_st_t = Tok()
                for g in range(4):
                    kb.op(PE, lambda h, g=g: h.transpose(out=ps_f[:, g * P:(g + 1) * P], in_=wsn[:, g, :], identity=ident_f[:]),
                          R=[sp_tok, c_tok], W=[ps_f_t])
                kb.op(ACT, lambda h: h.copy(out=wsT[:].rearrange("p g t -> p (g t)"), in_=ps_f[:]), R=[ps_f_t], W=[sp_tok])
                gu = sb(m2, "gu", [P, 256], F32)
                gv = sb(m2, "gv", [P, 256], F32)
                junk = sb(m2, "junk2", [P, 256], F32)
                vn = sb(m2, "vn", [P, 256], BF16)
                ysg = sb(m2, "ysg", [P, 256], BF16)
                st4 = sb(m2, "st4", [P, 8], F32)
                gvt = sb(m2, "gvt", [P, 256], BF16)
                gsl = sb(m2, "gsl", [P, 256], F32)
                gaT = sb(m2, "gaT", [16, P], BF16)
                e1 = sb(m2, "e1", [P, P], F32)
                cs = sb(m2, "cs", [P, P], F32)
                ebt = sb(m2, "ebt", [P, P], F32)
                eit = sb(m2, "eit", [P, P], F32)
                qeT = sb(m2, "qeT", [P, P], BF16)
                keT = sb(m2, "keT", [P, P], BF16)
                ke_tm = sb(m2, "ke_tm", [P, P], BF16)
                scT = sb(m2, "scT", [P, 4, P], BF16)
                state = sb(m2, "state", [P, 64], F32)
                state_b = sb(m2, "state_b", [P, 4, 64], BF16)
                keT4 = sb(m2, "keT4", [P, 4, P], BF16)
                tmp4 = sb(m2, "tmp4", [P, 4, 64], F32)
                stmp = sb(m2, "stmp", [P, 64], F32)
                o_sb = sb(m2, "o_sb", [P, 256], F32)
                osq = sb(m2, "osq", [P, 256], F32)
                ygl = sb(m2, "ygl", [P, 256], BF16)
                T = {k_: Tok() for k_ in ["gu", "gv", "junk", "vn", "ysg", "st4", "gvt", "gsl", "gaT", "e1", "cs", "ebt",
                                          "eit", "qeT", "keT", "ke_tm", "scT", "state", "state_b", "stmp", "o_sb", "osq",
                                          "ygl", "keT4", "tmp4"]}
                kb.op(DVE, lambda h: h.memset(state[:], 0.0), W=[T["state"]])
                kb.op(DVE, lambda h: h.memset(state_b[:], 0.0), W=[T["state_b"]])
                for tt in range(NT):
                    tok = slice(tt * P, (tt + 1) * P)
                    ht = h_tok[tt]
                    for g in range(3):
                        for k in range(8):
                            kb.op(PE, lambda h, g=g, k=k: h.matmul(ps_tm[g][:], lhsT=hT[:, k, tok],
                                                                   rhs=wB[:, k, g * 512:(g + 1) * 512],
                                                                   start=(k == 0), stop=(k == 7)),
                                  R=[ht, w_tok], W=[ps_tm_t[g]])
                    for j in range(2):
                        for k in range(8):
                            kb.op(PE, lambda h, j=j, k=k: h.matmul(ps_f[:, j * P:(j + 1) * P], lhsT=wC[:, k, j * P:(j + 1) * P],
                                                                   rhs=hT[:, k, tok], start=(k == 0), stop=(k == 7)),
                                  R=[ht, w_tok], W=[ps_f_t])
                    for k in range(8):
                        kb.op(PE, lambda h, k=k: h.matmul(ps_g[0:16, 0:P], lhsT=wC[:, k, 256:272], rhs=hT[:, k, tok],
                                                          start=(k == 0), stop=(k == 7)), R=[ht, w_tok], W=[ps_g_t])
                    kb.op(ACT, lambda h: h.copy(out=v_sb[:, tt, :], in_=ps_tm[0][:]), R=[ps_tm_t[0]], W=[v_tok[tt]])
                    kb.op(ACT, lambda h: h.activation(out=gu[:], in_=ps_tm[1][:, 0:256], func=AF.Gelu_apprx_tanh),
                          R=[ps_tm_t[1]], W=[T["gu"]])
                    kb.op(ACT, lambda h: h.activation(out=gv[:], in_=ps_tm[1][:, 256:512], func=AF.Gelu_apprx_tanh),
                          R=[ps_tm_t[1]], W=[T["gv"]])
                    kb.op(ACT, lambda h: h.activation(out=junk[:], in_=gv[:], func=AF.Square, accum_out=st4[:, 0:1]),
                          R=[T["gv"]], W=[T["junk"], T["st4"]])
                    kb.op(DVE, lambda h: h.tensor_scalar(out=st4[:, 0:1], in0=st4[:, 0:1], scalar1=1.0 / 256, scalar2=EPS,
                                                         op0=ALU.mult, op1=ALU.add), R=[T["st4"]], W=[T["st4"]])
                    kb.op(ACT, lambda h: h.activation(out=st4[:, 0:1], in_=st4[:, 0:1], func=AF.Sqrt), R=[T["st4"]], W=[T["st4"]])
                    kb.op(DVE, lambda h: h.reciprocal(out=st4[:, 0:1], in_=st4[:, 0:1]), R=[T["st4"]], W=[T["st4"]])
                    kb.op(DVE, lambda h: h.scalar_tensor_tensor(out=vn[:], in0=gv[:], scalar=st4[:, 0:1], in1=sgg[:],
                                                                op0=ALU.mult, op1=ALU.mult),
                          R=[T["gv"], T["st4"], sp_tok], W=[T["vn"]])
                    kb.op(ACT, lambda h: h.copy(out=gaT[:], in_=ps_g[0:16, 0:P]), R=[ps_g_t], W=[T["gaT"]])
                    for g in range(4):
                        kb.op(PE, lambda h, g=g: h.matmul(ps_g[:, 128 + g * 64:128 + (g + 1) * 64], lhsT=wsT[:, g, :],
                                                          rhs=vn[:, g * 64:(g + 1) * 64], start=True, stop=True),
                              R=[T["vn"], sp_tok], W=[ps_g_t])
                    for g in range(4):
                        kb.op(DVE, lambda h, g=g: h.scalar_tensor_tensor(
                            out=ysg[:, g * 64:(g + 1) * 64], in0=ps_g[:, 128 + g * 64:128 + (g + 1) * 64],
                            scalar=bsT[:, g:g + 1], in1=gu[:, g * 64:(g + 1) * 64], op0=ALU.add, op1=ALU.mult),
                            R=[ps_g_t, T["gu"], sp_tok], W=[T["ysg"]])
                    kb.op(ACT, lambda h: h.copy(out=gvt[:], in_=ps_tm[2][:, 0:256]), R=[ps_tm_t[2]], W=[T["gvt"]])
                    kb.op(ACT, lambda h: h.activation(out=gsl[:], in_=ps_tm[2][:, 256:512], func=AF.Silu),
                          R=[ps_tm_t[2]], W=[T["gsl"]])
                    kb.op(DVE, lambda h: h.tensor_tensor(out=gsl[:], in0=gsl[:], in1=glg[:], op=ALU.mult),
                          R=[T["gsl"], sp_tok], W=[T["gsl"]])
                    kb.op(PE, lambda h: h.matmul(ps_f[:, 256:384], lhsT=wgt[:, :], rhs=gaT[:, :], start=True, stop=True),
                          R=[T["gaT"], sp_tok], W=[ps_f_t])
                    kb.op(ACT, lambda h: h.activation(out=e1[:], in_=ps_f[:, 256:384], func=AF.Exp, bias=nbg[:, 0:1], scale=-1.0),
                          R=[ps_f_t, sp_tok], W=[T["e1"]])
                    kb.op(ACT, lambda h: h.activation(out=e1[:], in_=e1[:], func=AF.Ln, bias=1.0), R=[T["e1"]], W=[T["e1"]])
                    kb.op(DVE, lambda h: h.tensor_tensor_scan(out=cs[:], data0=ones_f[:], data1=e1[:], initial=0.0,
                                                              op0=ALU.mult, op1=ALU.add), R=[T["e1"], c_tok], W=[T["cs"]])
                    kb.op(ACT, lambda h: h.activation(out=ebt[:], in_=cs[:], func=AF.Exp, scale=-1.0 / 16), R=[T["cs"]], W=[T["ebt"]])
                    kb.op(ACT, lambda h: h.activation(out=eit[:], in_=cs[:], func=AF.Exp, scale=1.0 / 16), R=[T["cs"]], W=[T["eit"]])
                    kb.op(DVE, lambda h: h.scalar_tensor_tensor(out=qeT[:], in0=ps_f[:, 0:P], scalar=32.0 ** -0.5, in1=ebt[:],
                                                                op0=ALU.mult, op1=ALU.mult), R=[ps_f_t, T["ebt"]], W=[T["qeT"]])
                    kb.op(DVE, lambda h: h.tensor_tensor(out=keT[:], in0=ps_f[:, P:2 * P], in1=eit[:], op=ALU.mult),
                          R=[ps_f_t, T["eit"]], W=[T["keT"]])
                    for hh in range(4):
                        kb.op(DVE, lambda h, hh=hh: h.scalar_tensor_tensor(out=keT4[:, hh, :], in0=ps_f[:, P:2 * P], scalar=hm[:, hh:hh + 1],
                                                                           in1=eit[:], op0=ALU.mult, op1=ALU.mult),
                              R=[ps_f_t, T["eit"], c_tok], W=[T["keT4"]])
                    kb.op(PE, lambda h: h.transpose(out=ps_b[:, 0:P], in_=keT[:], identity=ident_b[:]),
                          R=[T["keT"], c_tok], W=[ps_b_t])
                    kb.op(ACT, lambda h: h.copy(out=ke_tm[:], in_=ps_b[:, 0:P]), R=[ps_b_t], W=[T["ke_tm"]])
                    for hh in range(4):
                        kb.op(PE, lambda h, hh=hh: h.matmul(ps_s[:, hh * P:(hh + 1) * P], lhsT=keT4[:, hh, :],
                                                            rhs=qeT[:, :], start=True, stop=True),
                              R=[T["keT4"], T["qeT"]], W=[ps_s_t])
                    kb.op(DVE, lambda h: h.tensor_tensor(out=scT[:], in0=ps_s[:].rearrange("p (g t) -> p g t", g=4),
                                                         in1=mge_f[:, None, :].to_broadcast([P, 4, P]), op=ALU.mult),
                          R=[ps_s_t, c_tok], W=[T["scT"]])
                    for hh in range(4):
                        kb.op(PE, lambda h, hh=hh: h.matmul(ps_st[:, hh * 64:(hh + 1) * 64], lhsT=scT[:, hh, :],
                                                            rhs=gvt[:, hh * 64:(hh + 1) * 64], start=True, stop=False),
                              R=[T["scT"], T["gvt"]], W=[ps_st_t])
                        kb.op(PE, lambda h, hh=hh: h.matmul(ps_st[:, hh * 64:(hh + 1) * 64], lhsT=qeT[:, :],
                                                            rhs=state_b[:, hh, :], start=False, stop=True),
                              R=[T["qeT"], T["state_b"]], W=[ps_st_t])
                    kb.op(ACT, lambda h: h.copy(out=o_sb[:], in_=ps_st[:, 0:256]), R=[ps_st_t], W=[T["o_sb"]])
                    kb.op(PE, lambda h: h.matmul(ps_st[:, 256:512], lhsT=ke_tm[:, :], rhs=gvt[:, :], start=True, stop=True),
                          R=[T["ke_tm"], T["gvt"]], W=[ps_st_t])
                    kb.op(DVE, lambda h: h.tensor_tensor(out=tmp4[:], in0=ps_st[:, 256:512].rearrange("p (g v) -> p g v", g=4),
                                                         in1=hm[:, :, None].to_broadcast([P, 4, 64]), op=ALU.mult),
                          R=[ps_st_t, c_tok], W=[T["tmp4"]])
                    kb.op(DVE, lambda h: h.tensor_reduce(out=stmp[:], in_=tmp4[:].rearrange("p g v -> p v g"), axis=AX.X, op=ALU.add),
                          R=[T["tmp4"]], W=[T["stmp"]])
                    kb.op(DVE, lambda h: h.tensor_tensor(out=stmp[:], in0=stmp[:], in1=state[:], op=ALU.add),
                          R=[T["stmp"], T["state"]], W=[T["stmp"]])
                    kb.op(DVE, lambda h: h.tensor_scalar(out=state[:], in0=stmp[:], scalar1=ebt[:, P - 1:P], scalar2=None,
                                                         op0=ALU.mult), R=[T["stmp"], T["ebt"]], W=[T["state"]])
                    kb.op(DVE, lambda h: h.tensor_tensor(out=state_b[:], in0=state[:, None, :].to_broadcast([P, 4, 64]),
                                                         in1=hm[:, :, None].to_broadcast([P, 4, 64]), op=ALU.mult),
                          R=[T["state"], c_tok], W=[T["state_b"]])
                    kb.op(DVE, lambda h: h.tensor_tensor(out=osq[:], in0=o_sb[:], in1=o_sb[:], op=ALU.mult),
                          R=[T["o_sb"]], W=[T["osq"]])
                    kb.op(DVE, lambda h: h.tensor_reduce(out=st4[:, 4:8], in_=osq[:].rearrange("p (g v) -> p g v", g=4),
                                                         axis=AX.X, op=ALU.add), R=[T["osq"]], W=[T["st4"]])
                    kb.op(DVE, lambda h: h.tensor_scalar(out=st4[:, 4:8], in0=st4[:, 4:8], scalar1=1.0 / 64, scalar2=EPS,
                                                         op0=ALU.mult, op1=ALU.add), R=[T["st4"]], W=[T["st4"]])
                    kb.op(ACT, lambda h: h.activation(out=st4[:, 4:8], in_=st4[:, 4:8], func=AF.Sqrt), R=[T["st4"]], W=[T["st4"]])
                    kb.op(DVE, lambda h: h.reciprocal(out=st4[:, 4:8], in_=st4[:, 4:8]), R=[T["st4"]], W=[T["st4"]])
                    kb.op(DVE, lambda h: h.tensor_tensor(out=osq[:].rearrange("p (g v) -> p g v", g=4),
                                                         in0=o_sb[:].rearrange("p (g v) -> p g v", g=4),
                                                         in1=st4[:, 4:8, None].to_broadcast([P, 4, 64]), op=ALU.mult),
                          R=[T["o_sb"], T["st4"]], W=[T["osq"]])
                    kb.op(DVE, lambda h: h.tensor_tensor(out=ygl[:], in0=osq[:], in1=gsl[:], op=ALU.mult),
                          R=[T["osq"], T["gsl"]], W=[T["ygl"]])
                    for j in range(2):
                        kb.op(PE, lambda h, j=j: h.transpose(out=ps_b[:, (2 + j) * P:(3 + j) * P], in_=ysg[:, j * P:(j + 1) * P],
                                                             identity=ident_b[:]), R=[T["ysg"], c_tok], W=[ps_b_t])
                    for j in range(2):
                        kb.op(PE, lambda h, j=j: h.transpose(out=ps_b[:, (4 + j) * P:(5 + j) * P], in_=ygl[:, j * P:(j + 1) * P],
                                                             identity=ident_b[:]), R=[T["ygl"], c_tok], W=[ps_b_t])
                    kb.op(ACT, lambda h: h.copy(out=hT[:, 4:8, tok], in_=ps_b[:, 2 * P:6 * P].rearrange("p (c t) -> p c t", c=4)),
                          R=[ps_b_t], W=[ht])
            kb.barrier()
            with contextlib.ExitStack() as m3:
                psz = [pst(m3, "psz%d" % i, [P, 512], F32) for i in range(4)]
                psz_t = toks(4)
                pstr = [pst(m3, "pstr%d" % i, [P, 1024], BF16) for i in range(2)]
                pstr_t = toks(2)
                psy = [pst(m3, "psy%d" % i, [P, 512], F32) for i in range(2)]
                psy_t = toks(2)
                NB = 2
                A = [sb(m3, "sbA%d" % i, [P, S], F32) for i in range(NB)]
                B = [sb(m3, "sbB%d" % i, [P, S], F32) for i in range(NB)]
                Wb = [sb(m3, "sbW%d" % i, [P, S], BF16) for i in range(NB)]
                WT = [sb(m3, "sbWT%d" % i, [P, NT, P], BF16) for i in range(NB)]
                ntot = [sb(m3, "ntot%d" % i, [P, 1], F32) for i in range(NB)]
                A_t, B_t, Wb_t, WT_t, nt_t = toks(NB), toks(NB), toks(NB), toks(NB), toks(NB)
                def sb_params(n):
                    qb, hd = n // 8, n % 8
                    nk = qb + 1
                    Sk = nk * P
                    return qb, hd, nk, Sk, (Sk + 511) // 512, n % NB, hd // 2, 4 + hd // 2, (hd % 2) * 64

                def sb_stage1(n):
                    qb, hd, nk, Sk, nbk, u, cq, ck, hp = sb_params(n)
                    for j in range(nbk):
                        w = min(512, Sk - j * 512)
                        kb.op(PE, lambda h, j=j, w=w: h.matmul(psz[j][:, 0:w], lhsT=qkT[hp:hp + 64, cq, qb * P:(qb + 1) * P],
                                                               rhs=qkT[hp:hp + 64, ck, j * 512:j * 512 + w], start=True, stop=True),
                              R=[qk_tok[cq], qk_tok[ck]], W=[psz_t[j]])
                    for j in range(nbk):
                        w = min(512, Sk - j * 512)
                        kb.op(ACT, lambda h, j=j, w=w: h.activation(out=A[u][:, j * 512:j * 512 + w], in_=psz[j][:, 0:w], func=AF.Exp),
                              R=[psz_t[j]], W=[A_t[u]])
                    kb.op(ACT, lambda h: h.activation(out=A[u][:, 0:Sk], in_=A[u][:, 0:Sk], func=AF.Ln, bias=1.0),
                          R=[A_t[u]], W=[A_t[u]])
                    kb.op(POOL, lambda h: h.tensor_tensor(out=A[u][:, qb * P:Sk], in0=A[u][:, qb * P:Sk], in1=mlt_f[:], op=ALU.mult),
                          R=[A_t[u], c_tok], W=[A_t[u]])
                    kb.op(DVE, lambda h: h.tensor_tensor_scan(out=B[u][:, 0:Sk], data0=ones_f[:, 0:1].to_broadcast([P, Sk]),
                                                              data1=A[u][:, 0:Sk], initial=0.0, op0=ALU.mult, op1=ALU.add),
                          R=[A_t[u], c_tok], W=[B_t[u]])
                    kb.op(DVE, lambda h: h.tensor_scalar(out=ntot[u][:], in0=B[u][:, Sk - 1:Sk], scalar1=-1.0, scalar2=None,
                                                         op0=ALU.mult), R=[B_t[u]], W=[nt_t[u]])
                    kb.op(DVE, lambda h: h.tensor_copy(out=A[u][:, 0:1], in_=psz[0][:, 0:1]), R=[psz_t[0], A_t[u]], W=[A_t[u]])
                    for j in range(nbk):
                        w = min(512, Sk - j * 512)
                        lo = 1 if j == 0 else 0
                        kb.op(DVE, lambda h, j=j, w=w, lo=lo: h.tensor_tensor(out=A[u][:, j * 512 + lo:j * 512 + w], in0=psz[j][:, lo:w],
                                                                             in1=B[u][:, j * 512 + lo - 1:j * 512 + w - 1], op=ALU.add),
                              R=[psz_t[j], A_t[u], B_t[u]], W=[A_t[u]])

                def sb_stage2(n):
                    qb, hd, nk, Sk, nbk, u, cq, ck, hp = sb_params(n)
                    kb.op(ACT, lambda h: h.activation(out=Wb[u][:, 0:Sk], in_=A[u][:, 0:Sk], func=AF.Exp, bias=ntot[u][:, 0:1]),
                          R=[A_t[u], nt_t[u]], W=[Wb_t[u]])
                    kb.op(POOL, lambda h: h.tensor_tensor(out=Wb[u][:, qb * P:Sk], in0=Wb[u][:, qb * P:Sk], in1=mlt_b[:], op=ALU.mult),
                          R=[Wb_t[u], c_tok], W=[Wb_t[u]])
                    for g0 in range(0, nk, 8):
                        gi = (g0 // 8) % 2
                        ng = min(8, nk - g0)
                        for kk in range(ng):
                            kb.op(PE, lambda h, kk=kk, g0=g0, gi=gi: h.transpose(out=pstr[gi][:, kk * P:(kk + 1) * P],
                                                                                in_=Wb[u][:, (g0 + kk) * P:(g0 + kk + 1) * P],
                                                                                identity=ident_b[:]),
                                  R=[Wb_t[u], c_tok], W=[pstr_t[gi]])
                        dst = WT[u][:, g0:g0 + ng, :].rearrange("p k t -> p (k t)")
                        kb.op(ACT, lambda h, dst=dst, gi=gi, ng=ng: h.copy(out=dst, in_=pstr[gi][:, 0:ng * P]),
                              R=[pstr_t[gi]], W=[WT_t[u]])
                    yi = n % 2
                    for kk in range(nk):
                        kb.op(PE, lambda h, kk=kk, yi=yi: h.matmul(psy[yi][hp:hp + 64, 0:P], lhsT=v_sb[:, kk, hd * 64:(hd + 1) * 64],
                                                                   rhs=WT[u][:, kk, :], start=(kk == 0), stop=(kk == nk - 1)),
                              R=[v_tok[kk], WT_t[u]], W=[psy_t[yi]])
                    kb.op(ACT, lambda h, yi=yi: h.copy(out=hT[hp:hp + 64, cq, qb * P:(qb + 1) * P], in_=psy[yi][hp:hp + 64, 0:P]),
                          R=[psy_t[yi]], W=[h_tok[qb]])

                NU = NT * 8
                sb_stage1(0)
                for n in range(NU):
                    if n + 1 < NU:
                        sb_stage1(n + 1)
                    sb_stage2(n)
            kb.barrier()
        with contextlib.ExitStack() as m4:
            wo = sb(m4, "wo", [P, 8, D], BF16)
            wo_t = Tok()
            kb.dma(POOL, wo[:], w_out_d[l].rearrange("(c p) n -> p c n", p=P), W=[wo_t])
            pso = [pst(m4, "pso%d" % i, [P, 512], F32) for i in range(4)]
            pso_t = toks(4)
            n = 0
            for tt in range(NT):
                for ng in range(2):
                    i = n % 4
                    n += 1
                    for k in range(8):
                        kb.op(PE, lambda h, k=k, i=i, ng=ng, tt=tt: h.matmul(pso[i][:], lhsT=hT[:, k, tt * P:(tt + 1) * P],
                                                                             rhs=wo[:, k, ng * 512:(ng + 1) * 512],
                                                                             start=(k == 0), stop=(k == 7)),
                              R=[h_tok[tt], wo_t], W=[pso_t[i]])
                    dst = resid[:, tt, ng * 512:(ng + 1) * 512]
                    kb.op(DVE, lambda h, dst=dst, i=i: h.tensor_tensor(out=dst, in0=pso[i][:], in1=dst, op=ALU.add),
                          R=[pso_t[i], r_tok[tt]], W=[r_tok[tt]])
            kb.barrier()

    def norm_seq(gain_vec_ap, hT, h_tok):
        load_gain(gain_vec_ap)
        with contextlib.ExitStack() as ns:
            xn = [sb(ns, "ns_xn%d" % i, [P, D], BF16) for i in range(2)]
            xn_t = toks(2)
            junk = sb(ns, "ns_junk", [P, D], BF16)
            junk_t = Tok()
            ptb = [pst(ns, "ns_pt%d" % i, [P, 1024], BF16) for i in range(2)]
            ptb_t = toks(2)
            for tt in range(NT):
                i = tt % 2
                norm_tile(resid[:, tt, :], r_tok[tt], tt, xn[i], xn_t[i], ptb[i], ptb_t[i],
                          hT[:, :, tt * P:(tt + 1) * P], h_tok[tt], junk, junk_t)
        kb.barrier()

    def cross_attn(l, b, hT, h_tok):
        with contextlib.ExitStack() as cs_:
            kT = sb(cs_, "x_kT", [P, 8, NMEM], BF16)
            vx = sb(cs_, "x_v", [P, 2, D], BF16)
            qT = sb(cs_, "x_qT", [P, 8, S], BF16)
            kv_t, q_t = Tok(), toks(4)
            with contextlib.ExitStack() as c1:
                wkv = sb(c1, "x_wkv", [P, 8, 2 * D], BF16)
                wq = sb(c1, "x_wq", [P, 8, D], BF16)
                wkv_t, wq_t = Tok(), Tok()
                kb.dma(POOL, wkv[:], w_ckv_d[l].rearrange("(c p) n -> p c n", p=P), W=[wkv_t])
                kb.dma(POOL, wq[:], w_cq_d[l].rearrange("(c p) n -> p c n", p=P), W=[wq_t])
                mT = sb(c1, "x_mT", [P, 8, NMEM], BF16)
                mT_t = toks(2)
                mt = [sb(c1, "x_mt%d" % i, [P, D], F32) for i in range(2)]
                mt_t = toks(2)
                xn = [sb(c1, "x_xn%d" % i, [P, D], BF16) for i in range(2)]
                xn_t = toks(2)
                junk = sb(c1, "x_junk", [P, D], BF16)
                junk_t = Tok()
                ptb = [pst(c1, "x_pt%d" % i, [P, 1024], BF16) for i in range(2)]
                ptb_t = toks(2)
                psx = [pst(c1, "x_ps%d" % i, [P, 512], F32) for i in range(2)]
                psx_t = toks(2)
                load_gain(mem_gain_d[l, :])
                for i in range(2):
                    kb.dma(SP, mt[i][:], mem_d[b, i * P:(i + 1) * P, :], W=[mt_t[i]])
                    norm_tile(mt[i][:], mt_t[i], i, xn[i], xn_t[i], ptb[i], ptb_t[i], mT[:, :, i * P:(i + 1) * P], mT_t[i],
                              junk, junk_t)
                n = 0
                for cc in range(8):
                    i = n % 2
                    n += 1
                    for k in range(8):
                        kb.op(PE, lambda h, k=k, i=i, cc=cc: h.matmul(psx[i][:, 0:NMEM], lhsT=wkv[:, k, cc * P:(cc + 1) * P],
                                                                      rhs=mT[:, k, :], start=(k == 0), stop=(k == 7)),
                              R=[wkv_t] + mT_t, W=[psx_t[i]])
                    kb.op(ACT, lambda h, i=i, cc=cc: h.copy(out=kT[:, cc, :], in_=psx[i][:, 0:NMEM]), R=[psx_t[i]], W=[kv_t])
                for mi in range(2):
                    for ng in range(2):
                        i = n % 2
                        n += 1
                        for k in range(8):
                            kb.op(PE, lambda h, k=k, i=i, mi=mi, ng=ng: h.matmul(psx[i][:], lhsT=mT[:, k, mi * P:(mi + 1) * P],
                                                                                 rhs=wkv[:, k, D + ng * 512:D + (ng + 1) * 512],
                                                                                 start=(k == 0), stop=(k == 7)),
                                  R=[wkv_t] + mT_t, W=[psx_t[i]])
                        kb.op(DVE, lambda h, i=i, mi=mi, ng=ng: h.tensor_copy(out=vx[:, mi, ng * 512:(ng + 1) * 512], in_=psx[i][:]),
                              R=[psx_t[i]], W=[kv_t])
                for cc in range(8):
                    for tg in range(4):
                        i = n % 2
                        n += 1
                        for k in range(8):
                            kb.op(PE, lambda h, k=k, i=i, cc=cc, tg=tg: h.matmul(psx[i][:], lhsT=wq[:, k, cc * P:(cc + 1) * P],
                                                                                 rhs=hT[:, k, tg * 512:(tg + 1) * 512],
                                                                                 start=(k == 0), stop=(k == 7)),
                                  R=[wq_t] + h_tok[tg * 4:tg * 4 + 4], W=[psx_t[i]])
                        dst = qT[:, cc, tg * 512:(tg + 1) * 512]
                        if n % 2 == 0:
                            kb.op(ACT, lambda h, dst=dst, i=i: h.mul(out=dst, in_=psx[i][:], mul=1.0 / 16), R=[psx_t[i]], W=[q_t[tg]])
                        else:
                            kb.op(DVE, lambda h, dst=dst, i=i: h.tensor_scalar(out=dst, in0=psx[i][:], scalar1=1.0 / 16, scalar2=None,
                                                                              op0=ALU.mult), R=[psx_t[i]], W=[q_t[tg]])
            kb.barrier()
            with contextlib.ExitStack() as c2:
                wo = sb(c2, "x_wo", [P, 8, D], BF16)
                wo_t = Tok()
                kb.dma(POOL, wo[:], w_co_d[l].rearrange("(c p) n -> p c n", p=P), W=[wo_t])
                pss = [pst(c2, "x_pss%d" % i, [P, 512], F32) for i in range(2)]
                pss_t = toks(2)
                pstb = pst(c2, "x_pstb", [P, 1024], BF16)
                pstb_t = Tok()
                pso = [pst(c2, "x_pso%d" % i, [P, 512], F32) for i in range(2)]
                pso_t = toks(2)
                psr = [pst(c2, "x_psr%d" % i, [P, 512], F32) for i in range(2)]
                psr_t = toks(2)
                pf = sb(c2, "x_p", [P, 4, NMEM], F32)
                pn = sb(c2, "x_pn", [P, 4, NMEM], BF16)
                pT = sb(c2, "x_pT", [P, 8, P], BF16)
                oT = sb(c2, "x_oT", [P, 8, P], BF16)
                sm = sb(c2, "x_sm", [P, 12], F32)
                pf_t, pn_t, pT_t, oT_t, sm_t = Tok(), Tok(), Tok(), Tok(), Tok()
                for tt in range(NT):
                    tok = slice(tt * P, (tt + 1) * P)
                    for hh in range(4):
                        for dc in range(2):
                            kb.op(PE, lambda h, hh=hh, dc=dc: h.matmul(pss[hh // 2][:, (hh % 2) * NMEM:(hh % 2 + 1) * NMEM],
                                                                       lhsT=qT[:, 2 * hh + dc, tok], rhs=kT[:, 2 * hh + dc, :],
                                                                       start=(dc == 0), stop=(dc == 1)),
                                  R=[q_t[tt // 4], kv_t], W=[pss_t[hh // 2]])
                    for i in range(2):
                        kb.op(DVE, lambda h, i=i: h.tensor_reduce(out=sm[:, 2 * i:2 * i + 2],
                                                                  in_=pss[i][:].rearrange("p (g m) -> p g m", g=2),
                                                                  axis=AX.X, op=ALU.max), R=[pss_t[i]], W=[sm_t])
                    kb.op(DVE, lambda h: h.tensor_scalar(out=sm[:, 4:8], in0=sm[:, 0:4], scalar1=-1.0, scalar2=None, op0=ALU.mult),
                          R=[sm_t], W=[sm_t])
                    for hh in range(4):
                        kb.op(ACT, lambda h, hh=hh: h.activation(out=pf[:, hh, :], in_=pss[hh // 2][:, (hh % 2) * NMEM:(hh % 2 + 1) * NMEM],
                                                                 func=AF.Exp, bias=sm[:, 4 + hh:5 + hh], accum_out=sm[:, 8 + hh:9 + hh]),
                              R=[pss_t[hh // 2], sm_t], W=[pf_t, sm_t])
                    kb.op(DVE, lambda h: h.reciprocal(out=sm[:, 8:12], in_=sm[:, 8:12]), R=[sm_t], W=[sm_t])
                    kb.op(DVE, lambda h: h.tensor_tensor(out=pn[:], in0=pf[:], in1=sm[:, 8:12, None].to_broadcast([P, 4, NMEM]),
                                                         op=ALU.mult), R=[pf_t, sm_t], W=[pn_t])
                    for hh in range(4):
                        for mc in range(2):
                            j = hh * 2 + mc
                            kb.op(PE, lambda h, hh=hh, mc=mc, j=j: h.transpose(out=pstb[:, j * P:(j + 1) * P],
                                                                               in_=pn[:, hh, mc * P:(mc + 1) * P], identity=ident_b[:]),
                                  R=[pn_t, c_tok], W=[pstb_t])
                    kb.op(ACT, lambda h: h.copy(out=pT[:].rearrange("p j t -> p (j t)"), in_=pstb[:]), R=[pstb_t], W=[pT_t])
                    for cc in range(8):
                        hh = cc // 2
                        for mc in range(2):
                            kb.op(PE, lambda h, cc=cc, hh=hh, mc=mc: h.matmul(pso[cc // 4][:, (cc % 4) * P:(cc % 4 + 1) * P],
                                                                              lhsT=vx[:, mc, cc * P:(cc + 1) * P], rhs=pT[:, hh * 2 + mc, :],
                                                                              start=(mc == 0), stop=(mc == 1)),
                                  R=[kv_t, pT_t], W=[pso_t[cc // 4]])
                    for i in range(2):
                        if i == 0:
                            kb.op(ACT, lambda h, i=i: h.copy(out=oT[:, 4 * i:4 * i + 4, :].rearrange("p c t -> p (c t)"), in_=pso[i][:]),
                                  R=[pso_t[i]], W=[oT_t])
                        else:
                            kb.op(DVE, lambda h, i=i: h.tensor_copy(out=oT[:, 4 * i:4 * i + 4, :].rearrange("p c t -> p (c t)"), in_=pso[i][:]),
                                  R=[pso_t[i]], W=[oT_t])
                    for ng in range(2):
                        for k in range(8):
                            kb.op(PE, lambda h, k=k, ng=ng: h.matmul(psr[ng][:], lhsT=oT[:, k, :], rhs=wo[:, k, ng * 512:(ng + 1) * 512],
                                                                     start=(k == 0), stop=(k == 7)), R=[oT_t, wo_t], W=[psr_t[ng]])
                        dst = resid[:, tt, ng * 512:(ng + 1) * 512]
                        kb.op(DVE, lambda h, dst=dst, ng=ng: h.tensor_tensor(out=dst, in0=psr[ng][:], in1=dst, op=ALU.add),
                              R=[psr_t[ng], r_tok[tt]], W=[r_tok[tt]])
            kb.barrier()

    def peer(l):
        load_gain(norm_ffn_d[l, :])
        with contextlib.ExitStack() as ps_:
            wpq = sb(ps_, "p_wq", [P, 8, D], BF16)
            wpq_t = Tok()
            kb.dma(POOL, wpq[:], peer_wq_d[l].rearrange("(c p) n -> p c n", p=P), W=[wpq_t])
            skT = sb(ps_, "p_skT", [P, 8, P], BF16)
            skn_stack = contextlib.ExitStack()
            skn = sb(skn_stack, "p_skn", [P, 8, P], F32)
            sk_t = Tok()
            with nc.allow_non_contiguous_dma(reason="sub keys"):
                for hh in range(8):
                    for pp in range(2):
                        kb.dma(SP, skn[:, hh, pp * 64:(pp + 1) * 64], peer_sk_d[l, hh, pp], W=[sk_t])
            ps4 = [pst(ps_, "p_ps%d" % i, [P, 512], F32) for i in range(4)]
            ps4_t = toks(4)
            psS = [pst(ps_, "p_psS%d" % i, [P, 512], F32) for i in range(2)]
            psS_t = toks(2)
            psA = pst(ps_, "p_psA", [P, 1024], BF16)
            psA_t = Tok()
            for hh in range(8):
                kb.op(PE, lambda h, hh=hh: h.transpose(out=ps4[hh // 4][:, (hh % 4) * P:(hh % 4 + 1) * P], in_=skn[:, hh, :],
                                                       identity=ident_f[:]), R=[sk_t, c_tok], W=[ps4_t[hh // 4]])
            for i in range(2):
                kb.op(ACT, lambda h, i=i: h.copy(out=skT[:, 4 * i:4 * i + 4, :].rearrange("p g n -> p (g n)"), in_=ps4[i][:]),
                      R=[ps4_t[i]], W=[sk_t])
            kb.barrier()
            skn_stack.close()
            xn = sb(ps_, "p_xn", [P, D], BF16)
            junk = sb(ps_, "p_junk", [P, D], BF16)
            hTt = sb(ps_, "p_hT", [P, 8, P], BF16)
            qpT = sb(ps_, "p_qpT", [P, 8, P], BF16)
            ssb = sb(ps_, "p_s", [P, 16, P], F32)
            top = sb(ps_, "p_top", [P, 16, 16], F32)
            c16 = sb(ps_, "p_c16", [P, 8, 16], F32)
            sm = sb(ps_, "p_sm", [P, 32], F32)
            IG = 16
            NIG = P // IG
            zb = [sb(ps_, "p_zb%d" % i, [P, IG, P], F32) for i in range(2)]
            eb = [sb(ps_, "p_eb%d" % i, [P, IG * P], BF16) for i in range(2)]
            gh = [sb(ps_, "p_gh%d" % i, [P, IG * P], BF16) for i in range(2)]
            zb_t, eb_t, gh_t = toks(2), toks(2), toks(2)
            cand = zb[0][:].rearrange("p i j -> p (i j)").rearrange("p (h c) -> p h c", h=8)
            G = sb(ps_, "p_G", [P, NEXP], BF16)
            G_t = toks(NIG)
            ut = [sb(ps_, "p_ut%d" % i, [P, 8, 512], BF16) for i in range(2)]
            vt = [sb(ps_, "p_vt%d" % i, [P, 4, D], BF16) for i in range(2)]
            ge = [sb(ps_, "p_ge%d" % i, [P, 512], BF16) for i in range(2)]
            Ab = [sb(ps_, "p_A%d" % i, [P, 512], BF16) for i in range(2)]
            AT = [sb(ps_, "p_AT%d" % i, [P, 4, P], BF16) for i in range(2)]
            ge_t, Ab_t, AT_t = toks(2), toks(2), toks(2)
            psA2 = pst(ps_, "p_psA2", [P, 1024], BF16)
            psAs = [psA, psA2]
            psAs_t = [psA_t, Tok()]
            names = ["xn", "junk", "hTt", "qpT", "ssb", "sm"]
            T = {k_: Tok() for k_ in names}
            top_t, c16_t, R_t, ej_t = toks(16), toks(8), toks(16), toks(8)
            ej = sb(ps_, "p_ej", [P, 8, 16], F32)
            T["cand"] = zb_t[0]
            ut_t, vt_t = toks(2), toks(2)
            for tt in range(PEER_TILES):
                norm_tile(resid[:, tt, :], r_tok[tt], tt, xn, T["xn"], psA, psA_t, hTt[:], T["hTt"], junk, T["junk"])
                for cc in range(8):
                    i = cc // 4
                    for k in range(8):
                        kb.op(PE, lambda h, cc=cc, k=k, i=i: h.matmul(ps4[i][:, (cc % 4) * P:(cc % 4 + 1) * P], lhsT=wpq[:, k, cc * P:(cc + 1) * P],
                                                                      rhs=hTt[:, k, :], start=(k == 0), stop=(k == 7)),
                              R=[wpq_t, T["hTt"]], W=[ps4_t[i]])
                for i in range(2):
                    kb.op(ACT, lambda h, i=i: h.copy(out=qpT[:, 4 * i:4 * i + 4, :].rearrange("p c t -> p (c t)"), in_=ps4[i][:]),
                          R=[ps4_t[i]], W=[T["qpT"]])
                for hh in range(8):
                    for pp in range(2):
                        g = pp * 8 + hh
                        kb.op(PE, lambda h, hh=hh, pp=pp, g=g: h.matmul(ps4[g // 4][:, (g % 4) * P:(g % 4 + 1) * P],
                                                                        lhsT=qpT[pp * 64:(pp + 1) * 64, hh, :],
                                                                        rhs=skT[pp * 64:(pp + 1) * 64, hh, :], start=True, stop=True),
                              R=[T["qpT"], sk_t], W=[ps4_t[g // 4]])
                for i in range(4):
                    kb.op(ACT, lambda h, i=i: h.copy(out=ssb[:, 4 * i:4 * i + 4, :].rearrange("p g n -> p (g n)"), in_=ps4[i][:]),
                          R=[ps4_t[i]], W=[T["ssb"]])
                if PEER_UPTO < 2:
                    continue
                scrA = zb[1][:]
                scrB = zb[1][:].rearrange("p i j -> p (i j)").rearrange("p (h c) -> p h c", h=8)
                for g in range(16):
                    kb.op(DVE, lambda h, g=g: h.max(out=top[:, g, 0:8], in_=ssb[:, g, :]), R=[T["ssb"]],
                          W=[top_t[g]] + ([zb_t[1]] if g == 0 else []))
                for g in range(16):
                    kb.op(DVE, lambda h, g=g: h.match_replace(out=scrA[:, g, :], in_to_replace=top[:, g, 0:8], in_values=ssb[:, g, :],
                                                              imm_value=NEG), R=[T["ssb"], top_t[g]], W=[R_t[g]])
                for g in range(16):
                    kb.op(DVE, lambda h, g=g: h.max(out=top[:, g, 8:16], in_=scrA[:, g, :]), R=[R_t[g]], W=[top_t[g]])
                kb.op(DVE, lambda h: h.tensor_tensor(out=cand.rearrange("p h (a b) -> p h a b", a=16),
                                                     in0=top[:, 0:8, :, None].to_broadcast([P, 8, 16, 16]),
                                                     in1=top[:, 8:16, None, :].to_broadcast([P, 8, 16, 16]), op=ALU.add),
                      R=top_t, W=[T["cand"]])
                for hh in range(8):
                    kb.op(DVE, lambda h, hh=hh: h.max(out=c16[:, hh, 0:8], in_=cand[:, hh, :]), R=[T["cand"]], W=[c16_t[hh]])
                for hh in range(8):
                    kb.op(DVE, lambda h, hh=hh: h.match_replace(out=scrB[:, hh, :], in_to_replace=c16[:, hh, 0:8], in_values=cand[:, hh, :],
                                                                imm_value=NEG), R=[T["cand"], c16_t[hh]], W=[R_t[2 * hh], R_t[2 * hh + 1]])
                for hh in range(8):
                    kb.op(DVE, lambda h, hh=hh: h.max(out=c16[:, hh, 8:16], in_=scrB[:, hh, :]), R=[R_t[2 * hh], R_t[2 * hh + 1]], W=[c16_t[hh]])
                kb.op(DVE, lambda h: h.tensor_copy(out=sm[:, 0:8], in_=c16[:, :, 15]), R=c16_t, W=[T["sm"]])
                kb.op(DVE, lambda h: h.tensor_scalar(out=sm[:, 8:16], in0=c16[:, :, 0], scalar1=-1.0, scalar2=None, op0=ALU.mult),
                      R=c16_t, W=[T["sm"]])
                for hh in range(8):
                    kb.op(ACT, lambda h, hh=hh: h.activation(out=ej[:, hh, :], in_=c16[:, hh, :], func=AF.Exp, bias=sm[:, 8 + hh:9 + hh],
                                                             accum_out=sm[:, 16 + hh:17 + hh]), R=[c16_t[hh], T["sm"]], W=[ej_t[hh], T["sm"]])
                kb.op(ACT, lambda h: h.activation(out=sm[:, 16:24], in_=sm[:, 16:24], func=AF.Ln), R=[T["sm"]], W=[T["sm"]])
                kb.op(DVE, lambda h: h.tensor_tensor(out=sm[:, 24:32], in0=sm[:, 8:16], in1=sm[:, 16:24], op=ALU.subtract),
                      R=[T["sm"]] + R_t, W=[T["sm"], zb_t[1]])
                if PEER_UPTO < 3:
                    continue
                NEG_ = NEXP // 512
                cnt = [0]

                units = [(ig_, hh_) for ig_ in range(NIG) for hh_ in range(8)]

                def unit_A(n):
                    ig, hh = units[n]
                    u = n % 2
                    kb.op(DVE, lambda h, hh=hh, ig=ig, u=u: h.tensor_tensor(
                        out=zb[u][:], in0=ssb[:, hh, ig * IG:(ig + 1) * IG, None].to_broadcast([P, IG, P]),
                        in1=ssb[:, 8 + hh, None, :].to_broadcast([P, IG, P]), op=ALU.add),
                        R=[T["ssb"]], W=[zb_t[u]])
                    kb.op(ACT, lambda h, hh=hh, u=u: h.activation(out=eb[u][:], in_=zb[u][:].rearrange("p i j -> p (i j)"), func=AF.Exp,
                                                                  bias=sm[:, 24 + hh:25 + hh]), R=[zb_t[u], T["sm"]], W=[eb_t[u]])

                def unit_C(n):
                    ig, hh = units[n]
                    u = n % 2
                    Gblk = G[:, ig * IG * P:(ig + 1) * IG * P]
                    if hh == 0:
                        kb.op(DVE, lambda h, hh=hh, Gblk=Gblk, u=u: h.scalar_tensor_tensor(
                            out=Gblk, in0=zb[u][:].rearrange("p i j -> p (i j)"), scalar=sm[:, hh:hh + 1], in1=eb[u][:],
                            op0=ALU.is_ge, op1=ALU.mult), R=[zb_t[u], eb_t[u], T["sm"]], W=[G_t[ig]])
                    else:
                        kb.op(DVE, lambda h, hh=hh, u=u: h.scalar_tensor_tensor(
                            out=gh[u][:], in0=zb[u][:].rearrange("p i j -> p (i j)"), scalar=sm[:, hh:hh + 1], in1=eb[u][:],
                            op0=ALU.is_ge, op1=ALU.mult), R=[zb_t[u], eb_t[u], T["sm"]], W=[gh_t[u]])
                        kb.op(DVE, lambda h, Gblk=Gblk, u=u: h.tensor_tensor(out=Gblk, in0=Gblk, in1=gh[u][:], op=ALU.add),
                              R=[gh_t[u], G_t[ig]], W=[G_t[ig]])

                nun = [0]

                def emit_units(k):
                    for _ in range(k):
                        n = nun[0]
                        if n >= len(units):
                            return
                        if n == 0:
                            unit_A(0)
                        if n + 1 < len(units):
                            unit_A(n + 1)
                        unit_C(n)
                        nun[0] += 1

                def stage_S(eg):
                    i = eg % 2
                    kb.dma(SP, ut[i][:], ut_d[l, :, eg * 512:(eg + 1) * 512].rearrange("(c p) e -> p c e", p=P), W=[ut_t[i]])
                    kb.dma(SP, vt[i][:], vb_d[l, eg * 512:(eg + 1) * 512, :].rearrange("(c p) n -> p c n", p=P), W=[vt_t[i]])
                    for k in range(8):
                        kb.op(PE, lambda h, k=k, i=i: h.matmul(psS[i][:], lhsT=hTt[:, k, :], rhs=ut[i][:, k, :], start=(k == 0), stop=(k == 7)),
                              R=[T["hTt"], ut_t[i]], W=[psS_t[i]])
                    kb.op(ACT, lambda h, i=i: h.activation(out=ge[i][:], in_=psS[i][:], func=AF.Gelu_apprx_tanh), R=[psS_t[i]], W=[ge_t[i]])
                    kb.op(DVE, lambda h, eg=eg, i=i: h.tensor_tensor(out=Ab[i][:], in0=ge[i][:], in1=G[:, eg * 512:(eg + 1) * 512], op=ALU.mult),
                          R=[ge_t[i], G_t[(eg * 512) // (IG * P)]], W=[Ab_t[i]])

                def stage_T(eg):
                    i = eg % 2
                    for c in range(4):
                        kb.op(PE, lambda h, c=c, i=i: h.transpose(out=psAs[i][:, c * P:(c + 1) * P], in_=Ab[i][:, c * P:(c + 1) * P], identity=ident_b[:]),
                              R=[Ab_t[i], c_tok], W=[psAs_t[i]])
                    kb.op(ACT, lambda h, i=i: h.copy(out=AT[i][:].rearrange("p c t -> p (c t)"), in_=psAs[i][:, 0:512]), R=[psAs_t[i]], W=[AT_t[i]])

                def stage_V(eg):
                    i = eg % 2
                    for c in range(4):
                        for ng in range(2):
                            kb.op(PE, lambda h, c=c, ng=ng, i=i, eg=eg: h.matmul(ps4[2 + ng][:], lhsT=AT[i][:, c, :], rhs=vt[i][:, c, ng * 512:(ng + 1) * 512],
                                                                                 start=(eg == 0 and c == 0), stop=(eg == NEG_ - 1 and c == 3)),
                                  R=[AT_t[i], vt_t[i]], W=[ps4_t[2 + ng]])

                emit_units(8)
                for step in range(NEG_ + 2):
                    emit_units(2)
                    if step >= 2:
                        stage_V(step - 2)
                    if 1 <= step <= NEG_:
                        stage_T(step - 1)
                    if step < NEG_:
                        stage_S(step)
                for ng in range(2):
                    dst = resid[:, tt, ng * 512:(ng + 1) * 512]
                    kb.op(DVE, lambda h, dst=dst, ng=ng: h.tensor_tensor(out=dst, in0=ps4[2 + ng][:], in1=dst, op=ALU.add),
                          R=[ps4_t[2 + ng], r_tok[tt]], W=[r_tok[tt]])
        kb.barrier()

    stage = (dbg or {}).get("stage", "full")
    need_peer = stage in ("full", "peer0", "prep", "peeronly")
    if need_peer and not (dbg or {}).get("noprep"):
        prep_peer_weights()
    for b in range(nseq):
        for q in range(4):
            kb.dma(SP, resid[:, q * 4:(q + 1) * 4, :], x_d[b, q * 512:(q + 1) * 512, :].rearrange("(t p) d -> p t d", p=P),
                   W=r_tok[q * 4:(q + 1) * 4])
        done = stage == "prep"
        if stage == "peeronly":
            peer(0)
            done = True
        for l in range(depth if not done else 0):
            with contextlib.ExitStack() as ls:
                hT = sb(ls, "hT", [P, 8, S], BF16)
                h_tok = toks(NT)
                norm_seq(norm_mix_d[l, :], hT, h_tok)
                mixer(l, hT, h_tok)
                if stage == "mix0":
                    done = True
                if not done:
                    norm_seq(norm_mem_d[l, :], hT, h_tok)
                    cross_attn(l, b, hT, h_tok)
                    if stage == "xattn0":
                        done = True
            if done:
                break
            peer(l)
            if stage == "peer0":
                done = True
                break
        if not done:
            with contextlib.ExitStack() as fs:
                fg = sb(fs, "fg", [P, D], F32)
                fg_t = Tok()
                junk = sb(fs, "f_junk", [P, D], BF16)
                junk_t = Tok()
                with nc.allow_non_contiguous_dma(reason="gain bcast"):
                    kb.dma(SP, fg[:], final_gain_d.partition_broadcast(P), W=[fg_t])
                for tt in range(NT):
                    col = rstd[:, tt:tt + 1]
                    kb.op(ACT, lambda h, tt=tt: h.activation(out=junk[:], in_=resid[:, tt, :], func=AF.Square, accum_out=ssq[:, tt:tt + 1]),
                          R=[r_tok[tt]], W=[junk_t, n_tok])
                    kb.op(DVE, lambda h, tt=tt, col=col: h.tensor_scalar(out=col, in0=ssq[:, tt:tt + 1], scalar1=1.0 / D, scalar2=EPS,
                                                                        op0=ALU.mult, op1=ALU.add), R=[n_tok], W=[n_tok])
                    kb.op(ACT, lambda h, col=col: h.activation(out=col, in_=col, func=AF.Sqrt), R=[n_tok], W=[n_tok])
                    kb.op(DVE, lambda h, col=col: h.reciprocal(out=col, in_=col), R=[n_tok], W=[n_tok])
                    kb.op(DVE, lambda h, tt=tt, col=col: h.scalar_tensor_tensor(out=resid[:, tt, :], in0=resid[:, tt, :], scalar=col, in1=fg[:],
                                                                               op0=ALU.mult, op1=ALU.mult),
                          R=[r_tok[tt], n_tok, fg_t], W=[r_tok[tt]])
        for q in range(4):
            kb.dma(SP, out_d[b, q * 512:(q + 1) * 512, :].rearrange("(t p) d -> p t d", p=P), resid[:, q * 4:(q + 1) * 4, :],
                   R=r_tok[q * 4:(q + 1) * 4])
        kb.barrier()
    kb.barrier()
    return nc, kb


def _consts():
    p = np.arange(P)[:, None]
    f = np.arange(P)[None, :]
    return {
        "c_ident": (p == f).astype(np.float32),
        "c_mlt": (f < p).astype(np.float32),
        "c_mle": (f <= p).astype(np.float32),
        "c_mge": (f >= p).astype(np.float32),
        "c_hm": (np.arange(P)[:, None] // 32 == np.arange(4)[None, :]).astype(np.float32),
    }


_CACHE = {}


def kernel(**inputs):
    ncores = 8
    nseq = 2
    if "nc" not in _CACHE:
        _CACHE["nc"] = build(nseq=nseq)[0]
    nc = _CACHE["nc"]
    cst = _consts()
    in_maps = []
    for c in range(ncores):
        m = {}
        for k, v in inputs.items():
            v = np.ascontiguousarray(np.asarray(v, dtype=np.float32))
            if k in ("x", "mem"):
                m[k] = np.ascontiguousarray(v[c * nseq:(c + 1) * nseq])
            else:
                m[k] = v
        m.update(cst)
        in_maps.append(m)
    res = run_bass_kernel_spmd(nc, in_maps, core_ids=list(range(ncores)))
    return np.concatenate([np.asarray(r["out"], dtype=np.float32) for r in res.results], axis=0)
```
